# Optimizing a Trainium2 kernel written in Bass

```python
import jax, jax.numpy as jnp
from jax import lax
import numpy as np

D_MODEL = 2048
BATCH = 4
SEQ = 2048
DEPTH = 1

CHUNK = 64
N_PAST_CHUNKS = 8
BAND = (N_PAST_CHUNKS + 1) * CHUNK
ATTN_WIDTH = D_MODEL // 2
ATTN_HEAD_DIM = 64
ATTN_HEADS = ATTN_WIDTH // ATTN_HEAD_DIM
MAX_REL = 4 * CHUNK
REC_WIDTH = D_MODEL - ATTN_WIDTH
REC_HEAD_DIM = 128
REC_HEADS = REC_WIDTH // REC_HEAD_DIM
MIX_WIDTH = ATTN_WIDTH + REC_WIDTH
IN_PROJ_WIDTH = 3 * ATTN_WIDTH + 4 * REC_WIDTH
D_FF = ((8 * D_MODEL // 3 + 255) // 256) * 256
ALPHA = (2 * DEPTH) ** 0.25
BETA = (8 * DEPTH) ** -0.25
EPS = 1e-5
N_MOD = 6

kernel_name = "hybrid_chunkattn_hgrn2_deepnorm_adaln"


def _layernorm(x, g=None, b=None):
    xf = x.astype(jnp.float32)
    mu = jnp.mean(xf, axis=-1, keepdims=True)
    var = jnp.mean(jnp.square(xf - mu), axis=-1, keepdims=True)
    y = (xf - mu) * lax.rsqrt(var + EPS)
    if g is not None:
        y = y * g.astype(jnp.float32) + b.astype(jnp.float32)
    return y.astype(x.dtype)


def _rmsnorm(x, g):
    xf = x.astype(jnp.float32)
    y = xf * lax.rsqrt(jnp.mean(jnp.square(xf), axis=-1, keepdims=True) + EPS)
    return y * g.astype(jnp.float32)


def _chunk_attention(q, k, v, rel_bias):
    B, T, H, Dh = q.shape
    n_chunks = T // CHUNK
    pad = N_PAST_CHUNKS * CHUNK
    k_pad = jnp.pad(k, ((0, 0), (pad, 0), (0, 0), (0, 0)))
    v_pad = jnp.pad(v, ((0, 0), (pad, 0), (0, 0), (0, 0)))
    rel = jnp.arange(CHUNK)[:, None] + pad - jnp.arange(BAND)[None, :]
    idx = jnp.clip(rel, -MAX_REL, MAX_REL) + MAX_REL
    bias = rel_bias[:, idx].astype(jnp.float32)
    q_chunks = q.reshape(B, n_chunks, CHUNK, H, Dh).transpose(1, 0, 2, 3, 4)
    scale = Dh ** -0.5
    band_pos = jnp.arange(BAND)

    def one_chunk(args):
        n, qc = args
        kb = lax.dynamic_slice_in_dim(k_pad, n * CHUNK, BAND, axis=1)
        vb = lax.dynamic_slice_in_dim(v_pad, n * CHUNK, BAND, axis=1)
        s = jnp.einsum('bthd,bjhd->bhtj', qc, kb).astype(jnp.float32) * scale + bias
        valid = band_pos >= (N_PAST_CHUNKS - n) * CHUNK
        s = jnp.where(valid, s, -jnp.inf)
        p = jax.nn.softmax(s, axis=-1).astype(vb.dtype)
        return jnp.einsum('bhtj,bjhd->bthd', p, vb)

    out = lax.map(one_chunk, (jnp.arange(n_chunks), q_chunks))
    return out.transpose(1, 0, 2, 3, 4).reshape(B, T, H, Dh)


def _hgrn2(q, f_logit, i, lower_bound):
    B, T, H, Dk = q.shape
    lb = lower_bound.reshape(H, Dk).astype(jnp.float32)
    f = lb + (1.0 - lb) * jax.nn.sigmoid(f_logit.astype(jnp.float32))
    log_f = jnp.log(f)
    k = 1.0 - f
    q = jax.nn.silu(q.astype(jnp.float32))
    i = i.astype(jnp.float32)
    Dv = i.shape[-1]
    n_chunks = T // CHUNK

    def to_chunks(a):
        return a.reshape(B, n_chunks, CHUNK, H, a.shape[-1]).transpose(1, 0, 3, 2, 4)

    causal = jnp.tril(jnp.ones((CHUNK, CHUNK), dtype=bool))[:, :, None]

    def step(S, inp):
        qc, kc, ic, gc = inp
        b = jnp.cumsum(gc, axis=2)
        diff = b[:, :, :, None, :] - b[:, :, None, :, :]
        decay = jnp.exp(jnp.where(causal, diff, -jnp.inf))
        scores = jnp.einsum('bhtd,bhtsd,bhsd->bhts', qc, decay, kc)
        o = (jnp.einsum('bhts,bhse->bhte', scores, ic)
             + jnp.einsum('bhtd,bhde->bhte', qc * jnp.exp(b), S))
        b_last = b[:, :, -1:, :]
        S = (jnp.exp(b_last[:, :, 0, :, None]) * S
             + jnp.einsum('bhsd,bhse->bhde', kc * jnp.exp(b_last - b), ic))
        return S, o

    S0 = jnp.zeros((B, H, Dk, Dv), jnp.float32)
    _, o = lax.scan(step, S0, (to_chunks(q), to_chunks(k), to_chunks(i), to_chunks(log_f)))
    return o.transpose(1, 0, 3, 2, 4).reshape(B, T, H, Dv)


def _token_mixer(h, w_in, rel_bias, attn_gain, lower_bound, gnorm_gain, w_o):
    B, T, _ = h.shape
    proj = h @ w_in
    splits = [ATTN_WIDTH, 2 * ATTN_WIDTH, 3 * ATTN_WIDTH,
              3 * ATTN_WIDTH + REC_WIDTH, 3 * ATTN_WIDTH + 2 * REC_WIDTH,
              3 * ATTN_WIDTH + 3 * REC_WIDTH]
    q_a, k_a, v_a, q_b, f_b, i_b, g_b = jnp.split(proj, splits, axis=-1)
    heads_a = lambda a: a.reshape(B, T, ATTN_HEADS, ATTN_HEAD_DIM)
    heads_b = lambda a: a.reshape(B, T, REC_HEADS, REC_HEAD_DIM)
    o_a = _chunk_attention(heads_a(q_a), heads_a(k_a), heads_a(v_a), rel_bias)
    o_a = _rmsnorm(o_a, attn_gain.reshape(ATTN_HEADS, ATTN_HEAD_DIM)).reshape(B, T, ATTN_WIDTH)
    o_b = _hgrn2(heads_b(q_b), heads_b(f_b), heads_b(i_b), lower_bound)
    o_b = _rmsnorm(o_b, gnorm_gain).reshape(B, T, REC_WIDTH)
    o_b = o_b * jax.nn.silu(g_b.astype(jnp.float32))
    out = jnp.concatenate([o_a, o_b], axis=-1).astype(h.dtype)
    return out @ w_o


def _swiglu(h, w_ffn_in, w_ffn_out):
    gate, up = jnp.split(h @ w_ffn_in, 2, axis=-1)
    return (jax.nn.silu(gate) * up) @ w_ffn_out


def setup_inputs(seed: int = 0) -> dict:
    key = jax.random.key(seed)
    ks = jax.random.split(key, 20)
    f32 = jnp.float32
    nrm = lambda k, shape, s: jax.random.normal(k, shape, f32) * s
    return {
        "x": nrm(ks[0], (BATCH, SEQ, D_MODEL), 1.0),
        "c": nrm(ks[1], (BATCH, D_MODEL), 1.0),
        "w_ada": nrm(ks[2], (DEPTH, D_MODEL, N_MOD * D_MODEL), 0.5 * D_MODEL ** -0.5),
        "b_ada": nrm(ks[3], (DEPTH, N_MOD * D_MODEL), 0.01),
        "w_in": nrm(ks[4], (DEPTH, D_MODEL, IN_PROJ_WIDTH), D_MODEL ** -0.5),
        "rel_bias": nrm(ks[5], (DEPTH, ATTN_HEADS, 2 * MAX_REL + 1), 0.1),
        "attn_norm_g": 1.0 + nrm(ks[6], (DEPTH, ATTN_WIDTH), 0.02),
        "lb_logits": nrm(ks[7], (DEPTH + 1, REC_WIDTH), 0.1),
        "gnorm_g": 1.0 + nrm(ks[8], (DEPTH, REC_HEAD_DIM), 0.02),
        "w_o": nrm(ks[9], (DEPTH, MIX_WIDTH, D_MODEL), BETA * MIX_WIDTH ** -0.5),
        "ln1_g": 1.0 + nrm(ks[10], (DEPTH, D_MODEL), 0.02),
        "ln1_b": nrm(ks[11], (DEPTH, D_MODEL), 0.01),
        "w_ffn_in": nrm(ks[12], (DEPTH, D_MODEL, 2 * D_FF), D_MODEL ** -0.5),
        "w_ffn_out": nrm(ks[13], (DEPTH, D_FF, D_MODEL), BETA * D_FF ** -0.5),
        "ln2_g": 1.0 + nrm(ks[14], (DEPTH, D_MODEL), 0.02),
        "ln2_b": nrm(ks[15], (DEPTH, D_MODEL), 0.01),
    }


def reference(x, c, w_ada, b_ada, w_in, rel_bias, attn_norm_g, lb_logits, gnorm_g, w_o,
              ln1_g, ln1_b, w_ffn_in, w_ffn_out, ln2_g, ln2_b):
    lower_bounds = jnp.cumsum(jax.nn.softmax(lb_logits.astype(jnp.float32), axis=0), axis=0)
    c_act = jax.nn.silu(c)
    for layer in range(DEPTH):
        mod = c_act @ w_ada[layer] + b_ada[layer]
        shift1, scale1, gate1, shift2, scale2, gate2 = [m[:, None, :] for m in jnp.split(mod, N_MOD, axis=-1)]
        h = _layernorm(x) * (1.0 + scale1) + shift1
        mix = _token_mixer(h, w_in[layer], rel_bias[layer], attn_norm_g[layer],
                           lower_bounds[layer], gnorm_g[layer], w_o[layer])
        x = _layernorm(ALPHA * x + gate1 * mix, ln1_g[layer], ln1_b[layer])
        h = _layernorm(x) * (1.0 + scale2) + shift2
        x = _layernorm(ALPHA * x + gate2 * _swiglu(h, w_ffn_in[layer], w_ffn_out[layer]),
                       ln2_g[layer], ln2_b[layer])
    return x
```

```python
import numpy as np
import concourse.bass as bass
import concourse.mybir as mybir
from concourse.bass_utils import run_bass_kernel_spmd

F32 = mybir.dt.float32
BF16 = mybir.dt.bfloat16
AF = mybir.ActivationFunctionType
ALU = mybir.AluOpType
AX = mybir.AxisListType

D = 2048
SEQ = 2048
NB = 4
TOK = 1024
NT = 16
DFF = 5632
EPS = 1e-5
ALPHA = 2.0 ** 0.25
NEG = -30000.0

ENGS = ("pe", "act", "dve", "pool", "sp")


class _Op:
    __slots__ = ("eng", "fn", "deps", "needed", "val", "dma", "dval")

    def __init__(self, eng, fn, deps, dma=None):
        self.eng = eng
        self.fn = fn
        self.deps = deps
        self.needed = False
        self.val = None
        self.dma = dma
        self.dval = None


class _DmaSem:
    def __init__(self, handle):
        self.handle = handle
        self.count = 0


class Prog:
    def __init__(self, nc):
        self.nc = nc
        self.ops = {e: [] for e in ENGS}
        self.reg = {}
        self.bar = {e: [] for e in ENGS}
        self.disabled = False

    def _deps_for(self, eng, reads, writes):
        deps = []
        for r in reads:
            st = self.reg.get(r)
            if st and st[0] is not None:
                deps.append(st[0])
        for w in writes:
            st = self.reg.get(w)
            if st:
                if st[0] is not None:
                    deps.append(st[0])
                deps.extend(st[1])
        return deps

    def _commit(self, op, reads, writes):
        for r in reads:
            st = self.reg.setdefault(r, [None, []])
            st[1].append(op)
        for w in writes:
            self.reg[w] = [op, []]

    def op(self, eng, fn, reads=(), writes=()):
        if self.disabled:
            return None
        deps = self._deps_for(eng, reads, writes)
        if self.bar[eng]:
            deps.extend(self.bar[eng])
            self.bar[eng] = []
        o = _Op(eng, fn, deps)
        self.ops[eng].append(o)
        self._commit(o, reads, writes)
        return o

    def dma(self, eng, sem, fn, reads=(), writes=()):
        if self.disabled:
            return None
        deps = self._deps_for(eng, reads, writes)
        if self.bar[eng]:
            deps.extend(self.bar[eng])
            self.bar[eng] = []
        o = _Op(eng, fn, deps, dma=sem)
        sem.count += 16
        o.dval = sem.count
        self.ops[eng].append(o)
        self._commit(o, reads, writes)
        return o

    def barrier(self, pool=False):
        last = []
        for e in ENGS:
            if self.ops[e]:
                for o in reversed(self.ops[e]):
                    if o.dma is None:
                        last.append(o)
                        break
        dl = {}
        for e in ENGS:
            for o in self.ops[e]:
                if o.dma is not None:
                    dl[id(o.dma)] = o
        last.extend(dl.values())
        for e in ENGS:
            if e == "pool" and not pool:
                continue
            self.bar[e] = list(last)
        if pool:
            self.reg = {}
        else:
            self.reg = {k: v for k, v in self.reg.items() if k.startswith("ring")}

    def finalize(self, block, sems):
        for e in ENGS:
            for o in self.ops[e]:
                for d in o.deps:
                    if d.dma is None:
                        if d.eng == "pe" and o.eng == "pe" and o.dma is None:
                            continue
                        d.needed = True
        for e in ENGS:
            c = 0
            for o in self.ops[e]:
                if o.dma is None and o.needed:
                    c += 1
                    o.val = c
        prog = self

        def emit(engname, engine):
            have = {}
            for o in prog.ops[engname]:
                for d in o.deps:
                    if d.dma is not None:
                        key = ("d", id(d.dma))
                        v = d.dval
                        h = d.dma.handle
                    else:
                        if d.eng == "pe" and engname == "pe" and o.dma is None:
                            continue
                        key = ("e", d.eng)
                        v = d.val
                        h = sems[d.eng]
                    if have.get(key, 0) >= v:
                        continue
                    have[key] = v
                    engine.wait_ge(h, v)
                ins = o.fn(engine)
                if o.dma is not None:
                    ins.then_inc(o.dma.handle, 16)
                elif o.needed:
                    ins.then_inc(sems[engname], 1)

        @block.tensor
        def _(eng):
            emit("pe", eng)

        @block.scalar
        def _(eng):
            emit("act", eng)

        @block.vector
        def _(eng):
            emit("dve", eng)

        @block.gpsimd
        def _(eng):
            emit("pool", eng)

        @block.sync
        def _(eng):
            emit("sp", eng)


def build_program(debug=None, stop_after=None):
    from contextlib import ExitStack
    nc = bass.Bass("TRN2", target_bir_lowering=False)
    dt = nc.dram_tensor
    xs = dt("xs", [SEQ, D], F32, kind="ExternalInput").ap()
    validd = dt("valid", [128, NT], F32, kind="ExternalInput").ap()
    cfm = dt("cfm", [128, 16], F32, kind="ExternalInput").ap()
    badafm = dt("badafm", [128, 96], F32, kind="ExternalInput").ap()
    lbfm = dt("lbfm", [128, 16], F32, kind="ExternalInput").ap()
    w_ada = dt("w_ada", [D, 6 * D], F32, kind="ExternalInput").ap()
    w_in = dt("w_in", [D, 7168], F32, kind="ExternalInput").ap()
    w_o = dt("w_o", [D, D], F32, kind="ExternalInput").ap()
    w_f1 = dt("w_f1", [D, 2 * DFF], F32, kind="ExternalInput").ap()
    w_f2 = dt("w_f2", [DFF, D], F32, kind="ExternalInput").ap()
    biasd = dt("biastab", [16, 128, 640], F32, kind="ExternalInput").ap()
    attng = dt("attng", [128, 1024], F32, kind="ExternalInput").ap()
    gng = dt("gng", [128, 128], F32, kind="ExternalInput").ap()
    ln1g = dt("ln1g", [128, D], F32, kind="ExternalInput").ap()
    ln1b = dt("ln1b", [128, D], F32, kind="ExternalInput").ap()
    ln2g = dt("ln2g", [128, D], F32, kind="ExternalInput").ap()
    ln2b = dt("ln2b", [128, D], F32, kind="ExternalInput").ap()
    cmaskd = dt("cmask", [128, 128], F32, kind="ExternalInput").ap()
    rmaskd = dt("rmask", [128, 512], F32, kind="ExternalInput").ap()
    identd = dt("ident", [128, 128], F32, kind="ExternalInput").ap()
    outd = dt("out", [TOK, D], F32, kind="ExternalOutput").ap()
    zscr = dt("zscr", [TOK, D], F32, kind="Internal").ap()
    dbg = {}
    if debug:
        for name, shape in debug.items():
            dbg[name] = dt("dbg_" + name, shape, F32, kind="ExternalOutput").ap()

    P = Prog(nc)
    es = ExitStack()
    with es:
        def sb(name, shape, dtype, stack=es):
            return stack.enter_context(nc.sbuf_tensor("sb_" + name, shape, dtype))

        def psum(name, shape, dtype, stack=es):
            return stack.enter_context(nc.psum_tensor("pp_" + name, shape, dtype))

        sems = {e: es.enter_context(nc.semaphore("s_" + e)) for e in ENGS}
        _dsn = [0]

        def dsem():
            _dsn[0] += 1
            return _DmaSem(es.enter_context(nc.semaphore("d%d" % _dsn[0])))

        RS = 3
        ringbuf = sb("ringbuf", [128, RS * 8192], BF16)
        ring = [ringbuf[:, i * 8192:(i + 1) * 8192] for i in range(RS)]
        ringE = [ringbuf[:, i * 12288:(i + 1) * 12288] for i in range(2)]
        ring_sem = [dsem() for _ in range(RS)]
        ringE_sem = [dsem() for _ in range(2)]
        ringE_n = [0]
        ring_n = [0]
        consts_sem = dsem()
        ARENA_B = 136 * 1024
        ident_f = sb("ident_f", [128, 128], F32)
        ident_b = sb("ident_b", [128, 128], BF16)
        ones_f = sb("ones_f", [128, 128], F32)
        valid = sb("valid", [128, NT], F32)
        modfm = sb("modfm", [128, 96], F32)
        sc1p = sb("sc1p", [128, 16], F32)
        sc2p = sb("sc2p", [128, 16], F32)
        epst = sb("epst", [128, 1], F32)
        lbv = sb("lbv", [128, 8], F32)
        oml = sb("oml", [128, 8], F32)
        cmask = sb("cmask", [128, 128], F32)
        rmask = sb("rmask", [128, 512], F32)
        gng_t = sb("gng_t", [128, 128], F32)
        attn_t = sb("attn_t", [128, 1024], F32)
        st6 = [sb("st6_%d" % i, [128, 4, 6], F32) for i in range(8)]
        mvt = [sb("mv_%d" % i, [128, 2], F32) for i in range(8)]
        sdt = [sb("sd_%d" % i, [128, 1], F32) for i in range(8)]
        rst = [sb("rs_%d" % i, [128, 1], F32) for i in range(8)]
        nmt = [sb("nm_%d" % i, [128, 1], F32) for i in range(8)]

        sm4 = sb("sm4", [128, 16], F32)
        onb = sb("onb", [128, 256], BF16)
        ones4 = sb("ones4", [128, 4], F32)
        mhalf = sb("mhalf", [128, 4], F32)
        Sbf = [sb("Sbf%d" % i, [128, 256], BF16) for i in range(6)]
        ebt = sb("ebt", [128, 2, 32], F32)
        junk = sb("junk", [128, 128], F32)
        tmpy = [sb("tmpy%d" % i, [128, 256], F32) for i in range(2)]
        sb_dg0 = sb("dg0", [128, 128], F32)
        sb_dg1 = sb("dg1", [128, 128], F32)
        ps = [psum("ps%d" % i, [128, 512], F32) for i in range(7)]
        psT = psum("psT", [128, 1024], BF16)

        _an = [0]

        def alloc(stack, shape, dtype, side=None):
            _an[0] += 1
            kw = {"side": side} if side else {}
            t = stack.enter_context(nc.sbuf_tensor("sb_a%d" % _an[0], shape, dtype, **kw))
            return t[:]

        def carve(off, shape, dtype):
            n = 1
            for d_ in shape[1:]:
                n *= d_
            nbytes = n * (2 if dtype == BF16 else 4)
            assert off % 4 == 0 and off + nbytes <= ARENA_B, (off, nbytes)
            v = arena[:, off // 4:(off + nbytes) // 4]
            if dtype == BF16:
                v = v.bitcast(BF16)
            if len(shape) == 3:
                v = v.rearrange("p (a b) -> p a b", a=shape[1])
            elif len(shape) == 4:
                v = v.rearrange("p (a b c) -> p a b c", a=shape[1], b=shape[2])
            return v

        KB = 1024

        def cload(dst, src, key):
            P.dma("sp", consts_sem, lambda e, d=dst, s=src: e.dma_start(out=d, in_=s), writes=[key])

        def OP(eng, fn, reads=(), writes=()):
            return P.op(eng, fn, reads, writes)

        def MM(out, lhsT, rhs, start, stop, reads, writes):
            return P.op("pe", lambda e: e.matmul(out, lhsT, rhs, start=start, stop=stop), reads, writes)

        def TR(out, in_, idt, reads, writes):
            return P.op("pe", lambda e: e.transpose(out, in_, idt), reads, writes)

        def ACT(out, in_, func, reads, writes, scale=None, bias=None, accum=None):
            kw = {}
            if scale is not None:
                kw["scale"] = scale
            if bias is not None:
                kw["bias"] = bias
            if accum is not None:
                kw["accum_out"] = accum
            return P.op("act", lambda e: e.activation(out=out, in_=in_, func=func, **kw), reads, writes)

        def TT(out, in0, in1, op, reads, writes, eng="dve"):
            return P.op(eng, lambda e: e.tensor_tensor(out=out, in0=in0, in1=in1, op=op), reads, writes)

        def TS(out, in0, s1, s2, op0, op1, reads, writes, eng="dve"):
            if op1 is None:
                return P.op(eng, lambda e: e.tensor_scalar(out=out, in0=in0, scalar1=s1, scalar2=None, op0=op0), reads, writes)
            return P.op(eng, lambda e: e.tensor_scalar(out=out, in0=in0, scalar1=s1, scalar2=s2, op0=op0, op1=op1), reads, writes)

        def STT(out, in0, scalar, in1, op0, op1, reads, writes):
            return P.op("dve", lambda e: e.scalar_tensor_tensor(out=out, in0=in0, scalar=scalar, in1=in1, op0=op0, op1=op1), reads, writes)

        def COPY(eng, out, in_, reads, writes):
            if eng == "act":
                return P.op("act", lambda e: e.activation(out=out, in_=in_, func=AF.Copy), reads, writes)
            return P.op(eng, lambda e: e.tensor_copy(out=out, in_=in_), reads, writes)

        def DMA(eng, sem, out, in_, reads, writes):
            return P.dma(eng, sem, lambda e: e.dma_start(out=out, in_=in_), reads, writes)

        cload(ident_f[:], identd, "ident_f")
        cload(valid[:], validd, "valid")
        cload(cmask[:], cmaskd, "cmask")
        cload(rmask[:], rmaskd, "rmask")
        cload(gng_t[:], gng, "gng")
        cload(attn_t[:], attng, "attn_t")
        OP("pool", lambda e: e.memset(epst[:], EPS), writes=["epst"])
        OP("pool", lambda e: e.memset(ones_f[:], 1.0), writes=["ones_f"])
        OP("pool", lambda e: e.memset(mhalf[:], -0.5), writes=["mhalf"])

        def ring_load(pieces, nk=16):
            s = ring_n[0] % RS
            ring_n[0] += 1
            wtot = sum(p[1] for p in pieces)
            view = ring[s][:, 0:nk * wtot].rearrange("p (k n) -> p k n", k=nk)
            key = "ring%d" % s
            step = 4
            first = True
            for (d0, wd, W, s0) in pieces:
                Wv = W.rearrange("(k p) n -> p k n", p=128)
                for k0 in range(0, nk, step):
                    k1 = min(nk, k0 + step)
                    DMA("pool", ring_sem[s], view[:, k0:k1, d0:d0 + wd], Wv[:, k0:k1, s0:s0 + wd],
                        [], [key] if first else [])
                    first = False
            P.reg[key] = [P.ops["pool"][-1], []]
            return view, key

        def ringE_load(W, col0, width, nk):
            s = ringE_n[0] % 2
            ringE_n[0] += 1
            view = ringE[s][:, 0:nk * width].rearrange("p (k n) -> p k n", k=nk)
            key = "ringE%d" % s
            Wv = W.rearrange("(k p) n -> p k n", p=128)
            first = True
            for k0 in range(0, nk, 4):
                k1 = min(nk, k0 + 4)
                DMA("pool", ringE_sem[s], view[:, k0:k1, :], Wv[:, k0:k1, col0:col0 + width], [], [key] if first else [])
                first = False
            P.reg[key] = [P.ops["pool"][-1], []]
            return view, key

        def ln_stats(src, srckey, par):
            k = "ln%d" % par
            for q in range(4):
                OP("dve", lambda e, q=q: e.bn_stats(out=st6[par][:, q, :], in_=src[:, q * 512:(q + 1) * 512]),
                   [srckey], [k + "st%d" % q])
            OP("dve", lambda e: e.bn_aggr(out=mvt[par][:], in_=st6[par][:].rearrange("p a b -> p (a b)")),
               [k + "st%d" % q for q in range(4)], [k + "mv"])
            ACT(sdt[par][:], mvt[par][:, 1:2], AF.Sqrt, [k + "mv", "epst"], [k + "sd"], scale=1.0, bias=epst[:, 0:1])
            OP("dve", lambda e: e.reciprocal(out=rst[par][:], in_=sdt[par][:]), [k + "sd"], [k + "rs"])
            TS(nmt[par][:], mvt[par][:, 0:1], rst[par][:, 0:1], -1.0, ALU.mult, ALU.mult, [k + "mv", k + "rs"], [k + "nm"])
            return [k + "rs", k + "nm"]

        def ln_a(src, srckey, par):
            k = "ln%d" % par
            for q in range(4):
                OP("dve", lambda e, q=q: e.bn_stats(out=st6[par][:, q, :], in_=src[:, q * 512:(q + 1) * 512]),
                   [srckey], [k + "st%d" % q])
            OP("dve", lambda e: e.bn_aggr(out=mvt[par][:], in_=st6[par][:].rearrange("p a b -> p (a b)")),
               [k + "st%d" % q for q in range(4)], [k + "mv"])

        def ln_b(par):
            k = "ln%d" % par
            ACT(sdt[par][:], mvt[par][:, 1:2], AF.Sqrt, [k + "mv", "epst"], [k + "sd"], scale=1.0, bias=epst[:, 0:1])

        def ln_c(par):
            k = "ln%d" % par
            OP("dve", lambda e: e.reciprocal(out=rst[par][:], in_=sdt[par][:]), [k + "sd"], [k + "rs"])
            TS(nmt[par][:], mvt[par][:, 0:1], rst[par][:, 0:1], -1.0, ALU.mult, ALU.mult, [k + "mv", k + "rs"], [k + "nm"])
            return [k + "rs", k + "nm"]

        def bc_from_fm(dst, dstkey, col0):
            dg = [sb_dg0, sb_dg1]
            for c in range(16):
                b_ = c % 2
                TS(dg[b_][:], ident_f[:], modfm[:, col0 + c:col0 + c + 1], None, ALU.mult, None,
                   ["ident_f", "modfm"], ["dg%d" % b_])
                MM(ps[b_][:, 0:128], ones_f[:], dg[b_][:], True, True, ["ones_f", "dg%d" % b_], ["psb%d" % b_])
                COPY("act", dst[:, c * 128:(c + 1) * 128], ps[b_][:, 0:128], ["psb%d" % b_], [dstkey + ".%d" % c])

        c_f = sb("c_f", [128, 16], F32)
        c_act = sb("c_act", [128, 16], BF16)
        bada = sb("bada", [128, 96], F32)
        lbl = sb("lbl", [128, 16], F32)
        cload(c_f[:], cfm, "c_f")
        cload(bada[:], badafm, "bada")
        cload(lbl[:], lbfm, "lbl")
        P.barrier()
        OP("dve", lambda e: e.tensor_copy(out=ident_b[:], in_=ident_f[:]), ["ident_f"], ["ident_b"])
        ACT(c_act[:], c_f[:], AF.Silu, ["c_f"], ["c_act"])
        TT(lbl[:, 0:8], lbl[:, 0:8], lbl[:, 8:16], ALU.subtract, ["lbl"], ["lbl"])
        ACT(lbv[:], lbl[:, 0:8], AF.Sigmoid, ["lbl"], ["lbv"])
        TS(oml[:], lbv[:], -1.0, 1.0, ALU.mult, ALU.add, ["lbv"], ["oml"])
        psm = ps[6]

        def ada_block(nb, dst=None, dkey="psm", c0=0):
            view, key = ring_load([(0, 512, w_ada, nb * 512)])
            dst = psm if dst is None else dst
            for j in range(4):
                col = nb * 4 + j - c0
                for k in range(16):
                    MM(dst[:, col:col + 1], view[:, k, j * 128:(j + 1) * 128], c_act[:, k:k + 1], k == 0, k == 15,
                       [key, "c_act"], [dkey])
        for nb in range(8):
            ada_block(nb)
        TT(modfm[:, 0:32], psm[:, 0:32], bada[:, 0:32], ALU.add, ["psm", "bada"], ["modfm"])
        TS(sc1p[:], modfm[:, 16:32], 1.0, None, ALU.add, None, ["modfm"], ["sc1p"])
        P.barrier()

        if stop_after == "0":
            P.disabled = True
        es_hT = ExitStack()
        es_oT = ExitStack()
        hT = alloc(es_hT, [128, 16, 2048], BF16)
        oT = alloc(es_oT, [128, 16, TOK], BF16, side="right")
        GB = 96 * KB

        def ln_mod_transpose(src, srckey, par, dstT, dstkey, tok0, scp, shcol, do_ln=True, act_only=False):
            if do_ln:
                keys = ln_stats(src, srckey, par)
                ACT(src, src, AF.Identity, [srckey] + keys, [srckey], scale=rst[par][:, 0:1], bias=nmt[par][:, 0:1])
            for g4 in range(4):
                bank = ps[g4]
                bkey = "psA%d" % g4
                for c4 in range(4):
                    c = g4 * 4 + c4
                    TR(bank[:, c4 * 128:(c4 + 1) * 128], src[:, c * 128:(c + 1) * 128], ident_f[:],
                       [srckey, "ident_f"], [bkey])
                for c4 in range(4):
                    c = g4 * 4 + c4
                    o_ = dstT[:, c, tok0:tok0 + 128]
                    i_ = bank[:, c4 * 128:(c4 + 1) * 128]
                    if g4 % 2 == 0 or act_only:
                        ACT(o_, i_, AF.Identity, [bkey, "sc", "modfm"], [dstkey + ".%d" % c],
                            scale=scp[:, c:c + 1], bias=modfm[:, shcol + c:shcol + c + 1])
                    else:
                        TS(o_, i_, scp[:, c:c + 1], modfm[:, shcol + c:shcol + c + 1], ALU.mult, ALU.add,
                           [bkey, "sc", "modfm"], [dstkey + ".%d" % c])

        es_a = ExitStack()
        xb = [alloc(es_a, [128, 2048], F32) for i in range(4)]
        xsem = [dsem() for _ in range(4)]

        def a_s1(TT_):
            for T in TT_:
                b_ = T % 4
                DMA("sp", xsem[b_], xb[b_], xs[T * 128:(T + 1) * 128, :], [], ["xb%d" % b_])
            for T in TT_:
                ln_a(xb[T % 4], "xb%d" % (T % 4), T % 8)
            for T in TT_:
                ln_b(T % 8)
            for T in TT_:
                ln_c(T % 8)
            for T in TT_:
                b_, par = T % 4, T % 8
                k = "ln%d" % par
                ACT(xb[b_], xb[b_], AF.Identity, ["xb%d" % b_, k + "rs", k + "nm"], ["xb%d" % b_],
                    scale=rst[par][:, 0:1], bias=nmt[par][:, 0:1])

        def a_s2(TT_):
            for T in TT_:
                ln_mod_transpose(xb[T % 4], "xb%d" % (T % 4), 0, hT, "hT.%d" % T, T * 128, sc1p, 0, do_ln=False)

        for p in range(8):
            a_s1([2 * p, 2 * p + 1])
            if p >= 1:
                a_s2([2 * p - 2, 2 * p - 1])
        a_s2([14, 15])
        P.barrier()
        es_a.close()

        def hkeys(T0, T1, kc):
            return ["hT.%d.%d" % (T, kc) for T in range(T0, T1)]

        if stop_after == "A":
            P.disabled = True
        es_b1 = ExitStack()
        kT = alloc(es_b1, [128, 2, 1536], BF16)
        vaug = alloc(es_b1, [128, 12, 4, 65], BF16)
        bhi = alloc(es_b1, [128, 4, 640], BF16)
        blo = alloc(es_b1, [128, 4, 640], BF16)
        bstg = alloc(es_b1, [128, 640], F32)
        qTz = alloc(es_b1, [128, 4, 512], BF16)
        pT = [alloc(es_b1, [128, 4, 640], BF16) for i in range(2)]
        onba = [alloc(es_b1, [128, 256], BF16) for i in range(2)]
        o_sb = alloc(es_b1, [128, 4, 64], F32)
        o_sq = alloc(es_b1, [128, 4, 64], F32)
        bsem = dsem()
        OP("pool", lambda e: e.memset(ones4[:], 1.0), [], ["ones4"])
        for t in range(12):
            ACT(vaug[:, t, :, 64:65], ones4[:].unsqueeze(2), AF.Identity, ["ones4", "valid"], ["vaug1.%d" % t],
                scale=valid[:, t + 4:t + 5])
        OP("dve", lambda e: e.memset(qTz, 0.0), [], ["qTz.%d" % h for h in range(4)])
        for ga in range(4):
            kv, kvkey = ring_load([(0, 256, w_in, 1024 + ga * 256), (256, 256, w_in, 2048 + ga * 256)])
            qv, qkey = ring_load([(0, 256, w_in, ga * 256)])
            for h in range(4):
                DMA("sp", bsem, bstg, biasd[ga * 4 + h], [], ["bstg"])
                COPY("dve", bhi[:, h, :], bstg, ["bstg"], ["bhi.%d" % h])
                TT(blo[:, h, :], bstg, bhi[:, h, :], ALU.subtract, ["bstg", "bhi.%d" % h], ["blo.%d" % h])
            nproj = [0]

            def proj_bank():
                i = nproj[0] % 2
                nproj[0] += 1
                return ps[i], "psP%d" % i
            for tg in range(1, 4):
                for pr in range(2):
                    bank, bkey = proj_bank()
                    for kc in range(16):
                        MM(bank[:, 0:512], kv[:, kc, pr * 128:(pr + 1) * 128], hT[:, kc, tg * 512:(tg + 1) * 512],
                           kc == 0, kc == 15, [kvkey] + hkeys(tg * 4, tg * 4 + 4, kc), [bkey])
                    COPY("act" if pr == 0 else "dve", kT[:, pr, (tg - 1) * 512:tg * 512], bank[:, 0:512],
                         [bkey], ["kT.%d.%d" % (pr, tg)])
            for T in range(4, 16):
                bank, bkey = proj_bank()
                for kc in range(16):
                    MM(bank[:, 0:256], hT[:, kc, T * 128:(T + 1) * 128], kv[:, kc, 256:512], kc == 0, kc == 15,
                       [kvkey, "hT.%d.%d" % (T, kc)], [bkey])
                ACT(vaug[:, T - 4, :, 0:64], bank[:, 0:256].rearrange("p (h d) -> p h d", h=4), AF.Identity,
                    [bkey, "valid"], ["vaug.%d" % (T - 4)], scale=valid[:, T:T + 1])
            def att_scores(T, ti):
                pb = T % 2
                for h in range(4):
                    pr, hf = h // 2, h % 2
                    sbank = ps[2 + h % 2]
                    skey = "psS%d" % (h % 2)
                    bbank = ps[4 if hf == 0 else 6]
                    bkey_ = "psB%d" % hf
                    MM(sbank[:, 0:512], ident_b[:], bhi[:, h, 0:512], True, False, ["ident_b", "bhi.%d" % h], [skey])
                    MM(sbank[:, 0:512], ident_b[:], blo[:, h, 0:512], False, False, ["ident_b", "blo.%d" % h], [skey])
                    for jb in range(4):
                        off = (T - 8 + jb) * 128
                        tgk = 1 + off // 512
                        MM(sbank[:, jb * 128:(jb + 1) * 128], kT[:, pr, off:off + 128], qTz[:, h, ti * 128:(ti + 1) * 128],
                           False, jb == 3, ["kT.%d.%d" % (pr, tgk), "qTz.%d" % h], [skey])
                    off = (T - 8 + 4) * 128
                    tgk = 1 + off // 512
                    bo = bbank[:, pr * 128:(pr + 1) * 128]
                    MM(bo, ident_b[:], bhi[:, h, 512:640], True, False, ["ident_b", "bhi.%d" % h], [bkey_])
                    MM(bo, ident_b[:], blo[:, h, 512:640], False, False, ["ident_b", "blo.%d" % h], [bkey_])
                    MM(bo, kT[:, pr, off:off + 128], qTz[:, h, ti * 128:(ti + 1) * 128], False, True,
                       ["kT.%d.%d" % (pr, tgk), "qTz.%d" % h], [bkey_])
                    ACT(pT[pb][:, h, 0:512], sbank[:, 0:512], AF.Exp, [skey], ["pT%d.%d" % (pb, h)])
                for hf in range(2):
                    ACT(pT[pb][:, hf:4:2, 512:640], ps[4 if hf == 0 else 6][:, 0:256].rearrange("p (h t) -> p h t", h=2),
                        AF.Exp, ["psB%d" % hf], ["pTb%d.%d" % (pb, hf), "pTb%d.%d" % (pb, hf + 2)])

            def att_pv(T):
                pb = T % 2
                for h in range(4):
                    for jb in range(5):
                        MM(ps[5][:, h * 65:(h + 1) * 65], pT[pb][:, h, jb * 128:(jb + 1) * 128], vaug[:, T - 8 + jb, h, :],
                           jb == 0, jb == 4,
                           ["pT%d.%d" % (pb, h), "pTb%d.%d" % (pb, h), "vaug.%d" % (T - 8 + jb), "vaug1.%d" % (T - 8 + jb)],
                           ["psO"])
                pso = ps[5][:, 0:260].rearrange("p (h d) -> p h d", h=4)
                OP("dve", lambda e, pso=pso: e.reciprocal(out=sm4[:, 0:4], in_=pso[:, :, 64]), ["psO"], ["sm4a"])
                TT(o_sb, pso[:, :, 0:64], sm4[:, 0:4].unsqueeze(2).to_broadcast([128, 4, 64]), ALU.mult,
                   ["psO", "sm4a"], ["o_sb"])
                TT(o_sq, o_sb, o_sb, ALU.mult, ["o_sb"], ["o_sq"])
                OP("dve", lambda e: e.tensor_reduce(out=sm4[:, 4:8], in_=o_sq, axis=AX.X, op=ALU.add), ["o_sq"], ["sm4b"])
                ACT(sm4[:, 8:12], sm4[:, 4:8], AF.Ln, ["sm4b", "epst"], ["sm4c"], scale=1.0 / 64, bias=epst[:, 0:1])
                ACT(sm4[:, 12:16], sm4[:, 8:12], AF.Exp, ["sm4c"], ["sm4d"], scale=-0.5)
                TT(o_sb, o_sb, sm4[:, 12:16].unsqueeze(2).to_broadcast([128, 4, 64]), ALU.mult, ["o_sb", "sm4d"], ["o_sb"])
                TT(onba[pb].rearrange("p (h d) -> p h d", h=4), o_sb,
                   attn_t[:, ga * 256:(ga + 1) * 256].rearrange("p (h d) -> p h d", h=4), ALU.mult,
                   ["o_sb", "attn_t"], ["onba%d" % pb])

            def att_tr(T):
                pb = T % 2
                for pr in range(2):
                    TR(psT[:, pr * 128:(pr + 1) * 128], onba[pb][:, pr * 128:(pr + 1) * 128], ident_b[:],
                       ["onba%d" % pb, "ident_b"], ["psT"])
                COPY("act", oT[:, ga * 2:ga * 2 + 2, (T - 8) * 128:(T - 7) * 128],
                     psT[:, 0:256].rearrange("p (a t) -> p a t", a=2), ["psT"], ["oT.%d.%d" % (ga, T - 8)])

            for tq in range(2):
                for pr in range(2):
                    bank, bkey = proj_bank()
                    for kc in range(16):
                        MM(bank[:, 0:512], qv[:, kc, pr * 128:(pr + 1) * 128],
                           hT[:, kc, 1024 + tq * 512:1024 + (tq + 1) * 512], kc == 0, kc == 15,
                           [qkey] + hkeys(8 + tq * 4, 12 + tq * 4, kc), [bkey])
                    for hf in range(2):
                        ACT(qTz[hf * 64:(hf + 1) * 64, 2 * pr + hf, :], bank[hf * 64:(hf + 1) * 64, 0:512], AF.Copy, [bkey],
                            ["qTz.%d" % (2 * pr + hf)], scale=0.125)
                for ti in range(4):
                    T = 8 + tq * 4 + ti
                    att_scores(T, ti)
                    if (T - 8) in (1, 4, 6):
                        ada_block(8 + ga * 3 + {1: 0, 4: 1, 6: 2}[T - 8], dst=ps[5][:, 384:448], dkey="psO", c0=32)
                    if T - 1 >= 8:
                        att_pv(T - 1)
                    if T - 2 >= 8:
                        att_tr(T - 2)
            att_pv(15)
            att_tr(14)
            att_tr(15)
        TT(modfm[:, 32:80], ps[5][:, 384:432], bada[:, 32:80], ALU.add, ["psO", "bada"], ["modfmB"])
        TS(sc2p[:], modfm[:, 64:80], 1.0, None, ALU.add, None, ["modfmB"], ["sc2p"])
        P.barrier()

        if stop_after == "B1":
            P.disabled = True
        es_b1.close()
        es_b2 = ExitStack()
        nkTm = alloc(es_b2, [128, 2, 1024], BF16)
        qtT = alloc(es_b2, [128, 2, 1024], BF16)
        nktok = alloc(es_b2, [128, 16, 256], BF16)
        w1 = [alloc(es_b2, [128, 512], F32) for i in range(2)]
        w2 = [alloc(es_b2, [128, 512], F32) for i in range(2)]
        w3 = [alloc(es_b2, [128, 512], F32) for i in range(2)]
        nkh = [alloc(es_b2, [128, 512], BF16) for i in range(2)]
        isb = [alloc(es_b2, [128, 256], BF16) for i in range(3)]
        sgt = [alloc(es_b2, [128, 256], F32) for i in range(3)]
        atsb = [alloc(es_b2, [128, 256], BF16) for i in range(2)]
        onb2 = [alloc(es_b2, [128, 256], BF16) for i in range(2)]
        Sst2 = [alloc(es_b2, [128, 256], F32) for i in range(2)]
        tmpS2 = [alloc(es_b2, [128, 256], F32) for i in range(2)]
        onf = alloc(es_b2, [128, 256], F32)
        def rec_loads(g):
            a = ring_load([(0, 256, w_in, 4096 + g * 256), (256, 256, w_in, 3072 + g * 256)])
            b = ring_load([(0, 256, w_in, 6144 + g * 256), (256, 256, w_in, 5120 + g * 256)])
            return a, b
        nxt_loads = rec_loads(0)
        for gr in range(4):
            (fq, fqkey), (gi, gikey) = nxt_loads
            npj = [0]

            def fbank():
                i = npj[0] % 2
                npj[0] += 1
                return ps[i], "psF%d" % i
            for tg in range(4):
                main = tg >= 2
                HH = range(2)
                tsl = slice(tg * 512, (tg + 1) * 512)
                nkd = [nkTm[:, hh, (tg - 2) * 512:(tg - 1) * 512] if main else nkh[hh] for hh in HH]
                nkk = ["nkTm.%d.%d" % (hh, tg) if main else "nkh%d" % hh for hh in HH]
                for hh in HH:
                    for kc in range(16):
                        MM(ps[hh][:, 0:512], fq[:, kc, hh * 128:(hh + 1) * 128], hT[:, kc, tsl],
                           kc == 0, kc == 15, [fqkey] + hkeys(tg * 4, tg * 4 + 4, kc), ["psF%d" % hh])
                for hh in HH:
                    ACT(w1[hh], ps[hh][:, 0:512], AF.Sigmoid, ["psF%d" % hh], ["w1.%d" % hh])
                for hh in HH:
                    hd = gr * 2 + hh
                    TS(w1[hh], w1[hh], oml[:, hd:hd + 1], lbv[:, hd:hd + 1], ALU.mult, ALU.add,
                       ["w1.%d" % hh, "oml", "lbv"], ["w1.%d" % hh])
                for hh in HH:
                    ACT(w2[hh], w1[hh], AF.Ln, ["w1.%d" % hh], ["w2.%d" % hh])
                for hh in HH:
                    OP("dve", lambda e, hh=hh: e.tensor_tensor_scan(out=w3[hh], data0=rmask[:], data1=w2[hh], initial=0.0,
                                                                   op0=ALU.mult, op1=ALU.add),
                       ["rmask", "w2.%d" % hh], ["w3.%d" % hh])
                for hh in HH:
                    ACT(w2[hh], w3[hh], AF.Exp, ["w3.%d" % hh], ["w2.%d" % hh], scale=-1.0)
                for hh in HH:
                    STT(nkd[hh], w1[hh], 1.0, w2[hh], ALU.subtract, ALU.mult, ["w1.%d" % hh, "w2.%d" % hh], [nkk[hh]])
                for hh in HH:
                    ACT(ebt[:, hh, tg * 8:(tg + 1) * 8], w3[hh][:, 63:512:64], AF.Exp, ["w3.%d" % hh],
                        ["ebt.%d.%d" % (hh, tg)])
                for hh in HH:
                    for j in range(4):
                        TR(psT[:, hh * 512 + j * 128:hh * 512 + (j + 1) * 128], nkd[hh][:, j * 128:(j + 1) * 128], ident_b[:],
                           [nkk[hh], "ident_b"], ["psT"])
                for hh in HH:
                    COPY("dve", nktok[:, tg * 4:(tg + 1) * 4, hh * 128:(hh + 1) * 128],
                         psT[:, hh * 512:(hh + 1) * 512].rearrange("p (j d) -> p j d", j=4), ["psT"],
                         ["nktok.%d.%d" % (tg, hh)])
                if main:
                    for hh in HH:
                        for kc in range(16):
                            MM(ps[hh][:, 0:512], fq[:, kc, 256 + hh * 128:256 + (hh + 1) * 128], hT[:, kc, tsl],
                               kc == 0, kc == 15, [fqkey] + hkeys(tg * 4, tg * 4 + 4, kc), ["psF%d" % hh])
                    for hh in HH:
                        ACT(w1[hh], ps[hh][:, 0:512], AF.Silu, ["psF%d" % hh], ["w1.%d" % hh])
                    for hh in HH:
                        ACT(w2[hh], w3[hh], AF.Exp, ["w3.%d" % hh], ["w2.%d" % hh])
                    for hh in HH:
                        TT(qtT[:, hh, (tg - 2) * 512:(tg - 1) * 512], w1[hh], w2[hh], ALU.mult,
                           ["w1.%d" % hh, "w2.%d" % hh], ["qtT.%d.%d" % (hh, tg)])
            if gr + 1 < 4:
                nxt_loads = rec_loads(gr + 1)
            OP("dve", lambda e: e.memset(Sst2[0], 0.0), [], ["S0.0", "S0.1"])
            OP("dve", lambda e: e.memset(Sbf[0][:], 0.0), [], ["Sbf0.0", "Sbf0.1"])

            def part_at(T):
                tgm = T // 4
                c0 = (T % 2) * 256
                for hh in range(2):
                    MM(ps[5][:, c0 + hh * 128:c0 + (hh + 1) * 128], nkTm[:, hh, (T - 8) * 128:(T - 7) * 128],
                       qtT[:, hh, (T - 8) * 128:(T - 7) * 128], True, True,
                       ["nkTm.%d.%d" % (hh, tgm), "qtT.%d.%d" % (hh, tgm)], ["psAT"])

            def part_mask(T):
                c0 = (T % 2) * 256
                TT(atsb[T % 2].rearrange("p (h t) -> p h t", h=2), ps[5][:, c0:c0 + 256].rearrange("p (h t) -> p h t", h=2),
                   cmask[:].unsqueeze(1).to_broadcast([128, 2, 128]), ALU.mult, ["psAT", "cmask"], ["atsb%d" % (T % 2)])

            def part_o(T):
                ib = isb[T % 3]
                ibk = "isb%d" % (T % 3)
                tgm = T // 4
                at = atsb[T % 2]
                ob_ = onb2[T % 2]
                p_ = T % 2
                pob = ps[6] if p_ == 0 else ps[0]
                pkey = "psOB" if p_ == 0 else "psF0"
                a0, t0_, r0 = 2 * p_, 4 + 2 * p_, 8 + 2 * p_
                for hh in range(2):
                    MM(pob[:, hh * 128:(hh + 1) * 128], at[:, hh * 128:(hh + 1) * 128], ib[:, hh * 128:(hh + 1) * 128],
                       True, False, ["atsb%d" % (T % 2), ibk], [pkey])
                    for c2 in range(2):
                        ci = 2 * T + c2
                        t0 = (T - 8) * 128 + c2 * 64
                        MM(pob[c2 * 64:(c2 + 1) * 64, hh * 128:(hh + 1) * 128], qtT[:, hh, t0:t0 + 64],
                           Sbf[ci % 6][:, hh * 128:(hh + 1) * 128], False, True,
                           ["qtT.%d.%d" % (hh, tgm), "Sbf%d.%d" % (ci % 6, hh)], [pkey])
                for hh in range(2):
                    ACT(junk[:], pob[:, hh * 128:(hh + 1) * 128], AF.Square, [pkey], ["junk", "sm4r%d.%d" % (p_, hh)],
                        accum=sm4[:, a0 + hh:a0 + hh + 1])
                TS(sm4[:, t0_:t0_ + 2], sm4[:, a0:a0 + 2], 1.0 / 128, EPS, ALU.mult, ALU.add,
                   ["sm4r%d.0" % p_, "sm4r%d.1" % p_], ["sm4s%d" % p_], eng="pool")
                TT(sm4[:, r0:r0 + 2], sm4[:, t0_:t0_ + 2], mhalf[:, 0:2], ALU.pow, ["sm4s%d" % p_, "mhalf"], ["sm4t%d" % p_],
                   eng="pool")
                for hh in range(2):
                    STT(onf[:, hh * 128:(hh + 1) * 128], pob[:, hh * 128:(hh + 1) * 128], sm4[:, r0 + hh:r0 + hh + 1], gng_t[:],
                        ALU.mult, ALU.mult, [pkey, "sm4t%d" % p_, "gng"], ["onf.%d" % hh])
                TT(ob_, onf, sgt[T % 3], ALU.mult, ["onf.0", "onf.1", "sgt%d" % (T % 3)], ["onb2_%d" % (T % 2)])

            def part_tr(T):
                ob_ = onb2[T % 2]
                for hh in range(2):
                    TR(psT[:, hh * 128:(hh + 1) * 128], ob_[:, hh * 128:(hh + 1) * 128], ident_b[:],
                       ["onb2_%d" % (T % 2), "ident_b"], ["psT"])
                COPY("act", oT[:, 8 + gr * 2:10 + gr * 2, (T - 8) * 128:(T - 7) * 128],
                     psT[:, 0:256].rearrange("p (a t) -> p a t", a=2), ["psT"], ["oT.%d.%d" % (8 + gr, T - 8)])

            def gi_proj(T):
                main = T >= 8
                bank = ps[2 + T % 2]
                bkey = "psGI%d" % (T % 2)
                n = 512 if main else 256
                rhs_lo = 0 if main else 256
                for kc in range(16):
                    MM(bank[:, 0:n], hT[:, kc, T * 128:(T + 1) * 128], gi[:, kc, rhs_lo:512], kc == 0, kc == 15,
                       [gikey, "hT.%d.%d" % (T, kc)], [bkey])
                if main:
                    part_at(T)
                ib = isb[T % 3]
                ibk = "isb%d" % (T % 3)
                ACT(ib, bank[:, n - 256:n], AF.Identity, [bkey, "valid"], [ibk], scale=valid[:, T:T + 1])
                if main:
                    ACT(sgt[T % 3], bank[:, 0:256], AF.Silu, [bkey], ["sgt%d" % (T % 3)])

            def gi_scan(T):
                ib = isb[T % 3]
                ibk = "isb%d" % (T % 3)
                for c2 in range(2):
                    ci = 2 * T + c2
                    dsb = ps[4] if c2 == 0 else ps[1]
                    dsl = dsb[:, 0:256]
                    for hh in range(2):
                        MM(dsb[:, hh * 128:(hh + 1) * 128],
                           nktok[c2 * 64:(c2 + 1) * 64, T, hh * 128:(hh + 1) * 128],
                           ib[c2 * 64:(c2 + 1) * 64, hh * 128:(hh + 1) * 128], True, True,
                           ["nktok.%d.%d" % (T // 4, hh), ibk], ["psDS0" if c2 == 0 else "psF1"])
                    tb = tmpS2[ci % 2]
                    tbk = "tmpS%d" % (ci % 2)
                    TT(tb.rearrange("p (h e) -> p h e", h=2), dsl.rearrange("p (h e) -> p h e", h=2),
                       ebt[:, :, ci:ci + 1].to_broadcast([128, 2, 128]), ALU.mult,
                       ["psDS0" if c2 == 0 else "psF1", "ebt.0.%d" % (T // 4), "ebt.1.%d" % (T // 4)], [tbk])
                    nxt = (ci + 1) % 6
                    for hh in range(2):
                        ebap = ebt[:, hh, ci:ci + 1]
                        Sin, Sout = Sst2[ci % 2], Sst2[(ci + 1) % 2]
                        STT(Sout[:, hh * 128:(hh + 1) * 128], Sin[:, hh * 128:(hh + 1) * 128], ebap,
                            tb[:, hh * 128:(hh + 1) * 128], ALU.mult, ALU.subtract,
                            ["S%d.%d" % (ci % 2, hh), tbk, "ebt.%d.%d" % (hh, T // 4)], ["S%d.%d" % ((ci + 1) % 2, hh)])
                        TS(Sbf[nxt][:, hh * 128:(hh + 1) * 128], Sout[:, hh * 128:(hh + 1) * 128], 1.0, 1.0,
                           ALU.mult, ALU.mult, ["S%d.%d" % ((ci + 1) % 2, hh)], ["Sbf%d.%d" % (nxt, hh)], eng="pool")

            gi_proj(0)
            for T in range(NT):
                if T + 1 < NT:
                    gi_proj(T + 1)
                gi_scan(T)
                if T >= 8:
                    part_mask(T)
                if T - 1 >= 8:
                    part_o(T - 1)
                if T - 2 >= 8:
                    part_tr(T - 2)
            part_o(NT - 1)
            part_tr(NT - 2)
            part_tr(NT - 1)
        P.barrier()
        if stop_after == "B2":
            P.disabled = True
        if debug:
            dsd = dsem()
            if "hT8" in dbg:
                DMA("pool", dsd, dbg["hT8"], hT[:, :, 1024:1152], [], [])
            if "oT" in dbg:
                DMA("pool", dsd, dbg["oT"], oT, [], [])
            P.barrier()
        es_b2.close()
        es_hT.close()
        es_x1 = ExitStack()
        es_c1 = ExitStack()
        x1p = alloc(es_x1, [128, 8, 2048], F32)
        g1bc = alloc(es_c1, [128, 2048], F32)
        xp = [alloc(es_c1, [128, 512], F32) for i in range(2)]
        xpsem = [dsem(), dsem()]
        bc_from_fm(g1bc, "g1bc", 32)
        n_ = 0
        for nb in range(4):
            wv, wkey = ring_load([(0, 512, w_o, nb * 512)])
            for T in range(8):
                b_ = n_ % 2
                n_ += 1
                DMA("sp", xpsem[b_], xp[b_], xs[1024 + T * 128:1024 + (T + 1) * 128, nb * 512:(nb + 1) * 512],
                    [], ["xp%d" % b_])
                bank, bkey = ps[2 + b_], "psC%d" % b_
                for kc in range(16):
                    MM(bank[:, 0:512], oT[:, kc, T * 128:(T + 1) * 128], wv[:, kc, :], kc == 0, kc == 15, [wkey], [bkey])
                dst = x1p[:, T, nb * 512:(nb + 1) * 512]
                dk = "x1p.%d.%d" % (T, nb)
                TT(dst, bank[:, 0:512], g1bc[:, nb * 512:(nb + 1) * 512], ALU.mult,
                   [bkey] + ["g1bc.%d" % c for c in range(nb * 4, nb * 4 + 4)], [dk])
                STT(dst, xp[b_], ALPHA, dst, ALU.mult, ALU.add, ["xp%d" % b_, dk], [dk])
        P.barrier()

        if stop_after == "C1":
            P.disabled = True
        es_c1.close()
        es_oT.close()
        es_c2 = ExitStack()
        es_h2 = ExitStack()
        ln1g_t = alloc(es_c2, [128, 2048], F32)
        ln1b_t = alloc(es_c2, [128, 2048], F32)
        xn2 = [alloc(es_c2, [128, 2048], F32) for i in range(2)]
        h2T = alloc(es_h2, [128, 16, TOK], BF16, side="right")
        lsem = dsem()
        DMA("sp", lsem, ln1g_t, ln1g, [], ["lnp"])
        DMA("sp", lsem, ln1b_t, ln1b, [], [])
        P.reg["lnp"] = [P.ops["sp"][-1], []]
        stsem = [dsem(), dsem()]
        for nb in range(20, 24):
            ada_block(nb)

        def ln_affine(src, skey, par, g_t, b_t, add_eng="dve"):
            keys = ln_stats(src, skey, par)
            ACT(src, src, AF.Identity, [skey] + keys, [skey], scale=rst[par][:, 0:1], bias=nmt[par][:, 0:1])
            TT(src, src, g_t, ALU.mult, [skey, "lnp"], [skey])
            TT(src, src, b_t, ALU.add, [skey, "lnp"], [skey], eng=add_eng)

        X1 = [(x1p[:, T, :], "x1.%d" % T) for T in range(8)]
        for T in range(8):
            ln_a(X1[T][0], X1[T][1], T)
        for T in range(8):
            ln_b(T)
        for T in range(8):
            ln_c(T)
        for T in range(8):
            src, skey = X1[T]
            k = "ln%d" % T
            ACT(src, src, AF.Identity, [skey, k + "rs", k + "nm"], [skey], scale=rst[T][:, 0:1], bias=nmt[T][:, 0:1])
            TT(src, src, ln1g_t, ALU.mult, [skey, "lnp"], [skey])
            TT(src, src, ln1b_t, ALU.add, [skey, "lnp"], [skey])
            DMA("sp", stsem[T % 2], zscr[T * 128:(T + 1) * 128, :], src, [skey], ["zs.%d" % T])
        for T in range(8):
            ln_a(X1[T][0], X1[T][1], T)
        for T in range(8):
            ln_b(T)
        for T in range(8):
            ln_c(T)
        for T in range(8):
            src, skey = X1[T]
            par = T % 2
            k = "ln%d" % T
            ACT(xn2[par], src, AF.Identity, [skey, k + "rs", k + "nm"], ["xn2_%d" % par], scale=rst[T][:, 0:1],
                bias=nmt[T][:, 0:1])
            ln_mod_transpose(xn2[par], "xn2_%d" % par, par, h2T, "h2T.%d" % T, T * 128, sc2p, 48, do_ln=False)
        TT(modfm[:, 80:96], psm[:, 80:96], bada[:, 80:96], ALU.add, ["psm", "bada"], ["modfmD"])
        P.barrier()

        if stop_after == "C2":
            P.disabled = True
        es_c2.close()
        es_x1.close()
        es_act = ExitStack()
        es_d = ExitStack()
        actT = alloc(es_act, [128, 44, TOK], BF16)
        sgb = [alloc(es_d, [128, 512], F32) for i in range(2)]
        n_ = 0
        for j2 in range(22):
            wv, wkey = ring_load([(0, 128, w_f1, (2 * j2) * 128), (128, 128, w_f1, DFF + (2 * j2) * 128),
                                  (256, 128, w_f1, (2 * j2 + 1) * 128), (384, 128, w_f1, DFF + (2 * j2 + 1) * 128)])
            for jj in range(2):
                j = 2 * j2 + jj
                for th in range(2):
                    bi = n_ % 2
                    n_ += 1
                    bG, kG = ps[2 * bi], "psG%d" % bi
                    bU, kU = ps[2 * bi + 1], "psU%d" % bi
                    for kc in range(16):
                        MM(bG[:, 0:512], wv[:, kc, jj * 256:jj * 256 + 128], h2T[:, kc, th * 512:(th + 1) * 512],
                           kc == 0, kc == 15, [wkey], [kG])
                    for kc in range(16):
                        MM(bU[:, 0:512], wv[:, kc, jj * 256 + 128:jj * 256 + 256], h2T[:, kc, th * 512:(th + 1) * 512],
                           kc == 0, kc == 15, [wkey], [kU])
                    ACT(sgb[bi], bG[:, 0:512], AF.Silu, [kG], ["sgb%d" % bi])
                    TT(actT[:, j, th * 512:(th + 1) * 512], sgb[bi], bU[:, 0:512], ALU.mult, ["sgb%d" % bi, kU],
                       ["actT.%d.%d" % (j, th)])
        P.barrier(pool=True)

        if stop_after == "D":
            P.disabled = True
        es_d.close()
        es_h2.close()
        es_e = ExitStack()
        g2bc = alloc(es_e, [128, 2048], F32, side="right")
        ln2g_t = alloc(es_e, [128, 2048], F32, side="right")
        ln2b_t = alloc(es_e, [128, 2048], F32, side="right")
        es_e1 = ExitStack()
        zb = [alloc(es_e1, [128, 8, 256], F32, side="right") for i in range(2)]
        DMA("sp", lsem, ln2g_t, ln2g, [], ["lnp"])
        DMA("sp", lsem, ln2b_t, ln2b, [], [])
        P.reg["lnp"] = [P.ops["sp"][-1], []]
        bc_from_fm(g2bc, "g2bc", 80)
        zv = zscr.rearrange("(t p) n -> p t n", p=128)
        zlsem = [dsem(), dsem()]
        zssem = [dsem(), dsem()]
        for nbk in range(8):
            wv, wkey = ringE_load(w_f2, nbk * 256, 256, 44)
            b_ = nbk % 2
            zkeys = ["zb%d.%d" % (b_, T) for T in range(8)]
            DMA("sp", zlsem[b_], zb[b_], zv[:, :, nbk * 256:(nbk + 1) * 256], [], zkeys)
            for T in range(8):
                bank, bkey = ps[2 + T % 2], "psY%d" % (T % 2)
                for kc in range(44):
                    MM(bank[:, 0:256], actT[:, kc, T * 128:(T + 1) * 128], wv[:, kc, :], kc == 0, kc == 43, [wkey], [bkey])
                TT(tmpy[T % 2][:], bank[:, 0:256], g2bc[:, nbk * 256:(nbk + 1) * 256], ALU.mult,
                   [bkey, "g2bc.%d" % (2 * nbk), "g2bc.%d" % (2 * nbk + 1)], ["tmpy%d" % (T % 2)])
                STT(zb[b_][:, T, :], zb[b_][:, T, :], ALPHA, tmpy[T % 2][:], ALU.mult, ALU.add,
                    ["tmpy%d" % (T % 2), zkeys[T]], [zkeys[T]])
            DMA("sp", zssem[b_], zv[:, :, nbk * 256:(nbk + 1) * 256], zb[b_], zkeys, [])
        P.barrier()
        es_e1.close()
        es_act.close()
        es_e2 = ExitStack()
        ob = [alloc(es_e2, [128, 2048], F32) for i in range(8)]
        olsem = [dsem() for _ in range(8)]
        ossem = [dsem() for _ in range(8)]
        for T in range(8):
            DMA("sp", olsem[T], ob[T], zscr[T * 128:(T + 1) * 128, :], [], ["ob%d" % T])
        for T in range(8):
            ln_a(ob[T], "ob%d" % T, T)
        for T in range(8):
            ln_b(T)
        for T in range(8):
            ln_c(T)
        for T in range(8):
            okey = "ob%d" % T
            k = "ln%d" % T
            ACT(ob[T], ob[T], AF.Identity, [okey, k + "rs", k + "nm"], [okey], scale=rst[T][:, 0:1], bias=nmt[T][:, 0:1])
            TT(ob[T], ob[T], ln2g_t, ALU.mult, [okey, "lnp"], [okey])
            TT(ob[T], ob[T], ln2b_t, ALU.add, [okey, "lnp"], [okey], eng=("pool" if T == 7 else "dve"))
            DMA("sp", ossem[T], outd[T * 128:(T + 1) * 128, :], ob[T], [okey], [])
        if stop_after is not None:
            P.disabled = False
            fsem = dsem()
            DMA("sp", fsem, outd[0:128, 0:128], ident_f[:], [], [])
        P.barrier()
        OP("sp", lambda e: e.nop())
        OP("act", lambda e: e.nop())

        with nc.Block() as block:
            P.finalize(block, sems)
        es_e2.close()
        es_e.close()
    return nc


def _host_inputs(inputs):
    x = np.asarray(inputs["x"], np.float32)
    c = np.asarray(inputs["c"], np.float32)
    rel = np.asarray(inputs["rel_bias"], np.float32)[0]
    jj = np.arange(128)[:, None]
    u = np.arange(640)[None, :]
    jb = u // 128
    tt = u % 128
    relidx = 512 + tt - jb * 128 - jj
    idx = np.clip(relidx, -256, 256) + 256
    kc = (jb * 128 + jj) // 64
    qc = 8 + tt // 64
    ok = (qc - kc >= 0) & (qc - kc <= 8)
    rel_ext = np.concatenate([rel, np.full((16, 1), NEG, np.float32)], axis=1)
    idx = np.where(ok, idx, 513)
    biastab = np.ascontiguousarray(rel_ext[:, idx])
    s = np.arange(128)[:, None]
    t = np.arange(128)[None, :]
    cmask = np.where((s // 64 == t // 64) & (s <= t), -1.0, 0.0).astype(np.float32)
    rmask = np.tile((np.arange(512) % 64 != 0).astype(np.float32)[None, :], (128, 1))
    ident = np.eye(128, dtype=np.float32)
    bc = lambda v, n: np.ascontiguousarray(np.broadcast_to(np.asarray(v, np.float32).reshape(1, n), (128, n)))
    lb = np.asarray(inputs["lb_logits"], np.float32)
    lbfm = np.concatenate([lb[0].reshape(8, 128).T, lb[1].reshape(8, 128).T], axis=1)
    shared = {
        "badafm": np.ascontiguousarray(np.asarray(inputs["b_ada"], np.float32)[0].reshape(96, 128).T),
        "lbfm": np.ascontiguousarray(lbfm),
        "w_ada": np.ascontiguousarray(np.asarray(inputs["w_ada"], np.float32)[0]),
        "w_in": np.ascontiguousarray(np.asarray(inputs["w_in"], np.float32)[0]),
        "w_o": np.ascontiguousarray(np.asarray(inputs["w_o"], np.float32)[0]),
        "w_f1": np.ascontiguousarray(np.asarray(inputs["w_ffn_in"], np.float32)[0]),
        "w_f2": np.ascontiguousarray(np.asarray(inputs["w_ffn_out"], np.float32)[0]),
        "biastab": biastab,
        "attng": bc(np.asarray(inputs["attn_norm_g"])[0], 1024),
        "gng": bc(np.asarray(inputs["gnorm_g"])[0], 128),
        "ln1g": bc(np.asarray(inputs["ln1_g"])[0], D),
        "ln1b": bc(np.asarray(inputs["ln1_b"])[0], D),
        "ln2g": bc(np.asarray(inputs["ln2_g"])[0], D),
        "ln2b": bc(np.asarray(inputs["ln2_b"])[0], D),
        "cmask": cmask, "rmask": rmask, "ident": ident,
    }
    maps = []
    for core in range(8):
        b, half = core // 2, core % 2
        if half == 0:
            xe = np.concatenate([np.zeros((1024, D), np.float32), x[b, 0:1024]], axis=0)
            v = np.concatenate([np.zeros((128, 8), np.float32), np.ones((128, 8), np.float32)], axis=1)
        else:
            xe = x[b]
            v = np.ones((128, NT), np.float32)
        m = dict(shared)
        m["xs"] = np.ascontiguousarray(xe)
        m["valid"] = np.ascontiguousarray(v)
        m["cfm"] = np.ascontiguousarray(c[b].reshape(16, 128).T)
        maps.append(m)
    return maps


def kernel(**inputs):
    maps = _host_inputs(inputs)
    nc = build_program()
    res = run_bass_kernel_spmd(nc, maps, core_ids=list(range(8)))
    out = np.zeros((NB, SEQ, D), np.float32)
    for core in range(8):
        b, half = core // 2, core % 2
        out[b, half * 1024:(half + 1) * 1024] = res.results[core]["out"]
    return out
```

```python
import numpy as np
import concourse.bass as bass
import concourse.mybir as mybir
from concourse.bass_utils import run_bass_kernel_spmd

F32 = mybir.dt.float32
BF16 = mybir.dt.bfloat16
AF = mybir.ActivationFunctionType
ALU = mybir.AluOpType
AX = mybir.AxisListType

D = 2048
SEQ = 2048
NB = 4
TOK = 1024
NT = 16
DFF = 5632
EPS = 1e-5
ALPHA = 2.0 ** 0.25
NEG = -30000.0

ENGS = ("pe", "act", "dve", "pool", "sp")


class _Op:
    __slots__ = ("eng", "fn", "deps", "needed", "val", "dma", "dval")

    def __init__(self, eng, fn, deps, dma=None):
        self.eng = eng
        self.fn = fn
        self.deps = deps
        self.needed = False
        self.val = None
        self.dma = dma
        self.dval = None


class _DmaSem:
    def __init__(self, handle):
        self.handle = handle
        self.count = 0


class Prog:
    def __init__(self, nc):
        self.nc = nc
        self.ops = {e: [] for e in ENGS}
        self.reg = {}
        self.bar = {e: [] for e in ENGS}
        self.disabled = False

    def _deps_for(self, eng, reads, writes):
        deps = []
        for r in reads:
            st = self.reg.get(r)
            if st and st[0] is not None:
                deps.append(st[0])
        for w in writes:
            st = self.reg.get(w)
            if st:
                if st[0] is not None:
                    deps.append(st[0])
                deps.extend(st[1])
        return deps

    def _commit(self, op, reads, writes):
        for r in reads:
            st = self.reg.setdefault(r, [None, []])
            st[1].append(op)
        for w in writes:
            self.reg[w] = [op, []]

    def op(self, eng, fn, reads=(), writes=()):
        if self.disabled:
            return None
        deps = self._deps_for(eng, reads, writes)
        if self.bar[eng]:
            deps.extend(self.bar[eng])
            self.bar[eng] = []
        o = _Op(eng, fn, deps)
        self.ops[eng].append(o)
        self._commit(o, reads, writes)
        return o

    def dma(self, eng, sem, fn, reads=(), writes=()):
        if self.disabled:
            return None
        deps = self._deps_for(eng, reads, writes)
        if self.bar[eng]:
            deps.extend(self.bar[eng])
            self.bar[eng] = []
        o = _Op(eng, fn, deps, dma=sem)
        sem.count += 16
        o.dval = sem.count
        self.ops[eng].append(o)
        self._commit(o, reads, writes)
        return o

    def barrier(self, pool=False):
        last = []
        for e in ENGS:
            if self.ops[e]:
                for o in reversed(self.ops[e]):
                    if o.dma is None:
                        last.append(o)
                        break
        dl = {}
        for e in ENGS:
            for o in self.ops[e]:
                if o.dma is not None:
                    dl[id(o.dma)] = o
        last.extend(dl.values())
        for e in ENGS:
            if e == "pool" and not pool:
                continue
            self.bar[e] = list(last)
        if pool:
            self.reg = {}
        else:
            self.reg = {k: v for k, v in self.reg.items() if k.startswith("ring")}

    def finalize(self, block, sems):
        for e in ENGS:
            for o in self.ops[e]:
                for d in o.deps:
                    if d.dma is None:
                        if d.eng == "pe" and o.eng == "pe" and o.dma is None:
                            continue
                        d.needed = True
        for e in ENGS:
            c = 0
            for o in self.ops[e]:
                if o.dma is None and o.needed:
                    c += 1
                    o.val = c
        prog = self

        def emit(engname, engine):
            have = {}
            for o in prog.ops[engname]:
                for d in o.deps:
                    if d.dma is not None:
                        key = ("d", id(d.dma))
                        v = d.dval
                        h = d.dma.handle
                    else:
                        if d.eng == "pe" and engname == "pe" and o.dma is None:
                            continue
                        key = ("e", d.eng)
                        v = d.val
                        h = sems[d.eng]
                    if have.get(key, 0) >= v:
                        continue
                    have[key] = v
                    engine.wait_ge(h, v)
                ins = o.fn(engine)
                if o.dma is not None:
                    ins.then_inc(o.dma.handle, 16)
                elif o.needed:
                    ins.then_inc(sems[engname], 1)

        @block.tensor
        def _(eng):
            emit("pe", eng)

        @block.scalar
        def _(eng):
            emit("act", eng)

        @block.vector
        def _(eng):
            emit("dve", eng)

        @block.gpsimd
        def _(eng):
            emit("pool", eng)

        @block.sync
        def _(eng):
            emit("sp", eng)


def build_program(debug=None, stop_after=None):
    from contextlib import ExitStack
    nc = bass.Bass("TRN2", target_bir_lowering=False)
    dt = nc.dram_tensor
    xs = dt("xs", [SEQ, D], F32, kind="ExternalInput").ap()
    validd = dt("valid", [128, NT], F32, kind="ExternalInput").ap()
    cfm = dt("cfm", [128, 16], F32, kind="ExternalInput").ap()
    badafm = dt("badafm", [128, 96], F32, kind="ExternalInput").ap()
    lbfm = dt("lbfm", [128, 16], F32, kind="ExternalInput").ap()
    w_ada = dt("w_ada", [D, 6 * D], F32, kind="ExternalInput").ap()
    w_in = dt("w_in", [D, 7168], F32, kind="ExternalInput").ap()
    w_o = dt("w_o", [D, D], F32, kind="ExternalInput").ap()
    w_f1 = dt("w_f1", [D, 2 * DFF], F32, kind="ExternalInput").ap()
    w_f2 = dt("w_f2", [DFF, D], F32, kind="ExternalInput").ap()
    biasd = dt("biastab", [16, 128, 640], F32, kind="ExternalInput").ap()
    attng = dt("attng", [128, 1024], F32, kind="ExternalInput").ap()
    gng = dt("gng", [128, 128], F32, kind="ExternalInput").ap()
    ln1g = dt("ln1g", [128, D], F32, kind="ExternalInput").ap()
    ln1b = dt("ln1b", [128, D], F32, kind="ExternalInput").ap()
    ln2g = dt("ln2g", [128, D], F32, kind="ExternalInput").ap()
    ln2b = dt("ln2b", [128, D], F32, kind="ExternalInput").ap()
    cmaskd = dt("cmask", [128, 128], F32, kind="ExternalInput").ap()
    rmaskd = dt("rmask", [128, 512], F32, kind="ExternalInput").ap()
    identd = dt("ident", [128, 128], F32, kind="ExternalInput").ap()
    outd = dt("out", [TOK, D], F32, kind="ExternalOutput").ap()
    zscr = dt("zscr", [TOK, D], F32, kind="Internal").ap()
    dbg = {}
    if debug:
        for name, shape in debug.items():
            dbg[name] = dt("dbg_" + name, shape, F32, kind="ExternalOutput").ap()

    P = Prog(nc)
    es = ExitStack()
    with es:
        def sb(name, shape, dtype, stack=es):
            return stack.enter_context(nc.sbuf_tensor("sb_" + name, shape, dtype))

        def psum(name, shape, dtype, stack=es):
            return stack.enter_context(nc.psum_tensor("pp_" + name, shape, dtype))

        sems = {e: es.enter_context(nc.semaphore("s_" + e)) for e in ENGS}
        _dsn = [0]

        def dsem():
            _dsn[0] += 1
            return _DmaSem(es.enter_context(nc.semaphore("d%d" % _dsn[0])))

        RS = 3
        ringbuf = sb("ringbuf", [128, RS * 8192], BF16)
        ring = [ringbuf[:, i * 8192:(i + 1) * 8192] for i in range(RS)]
        ringE = [ringbuf[:, i * 12288:(i + 1) * 12288] for i in range(2)]
        ring_sem = [dsem() for _ in range(RS)]
        ringE_sem = [dsem() for _ in range(2)]
        ringE_n = [0]
        ring_n = [0]
        consts_sem = dsem()
        ARENA_B = 136 * 1024
        ident_f = sb("ident_f", [128, 128], F32)
        ident_b = sb("ident_b", [128, 128], BF16)
        ones_f = sb("ones_f", [128, 128], F32)
        valid = sb("valid", [128, NT], F32)
        modfm = sb("modfm", [128, 96], F32)
        sc1p = sb("sc1p", [128, 16], F32)
        sc2p = sb("sc2p", [128, 16], F32)
        epst = sb("epst", [128, 1], F32)
        lbv = sb("lbv", [128, 8], F32)
        oml = sb("oml", [128, 8], F32)
        cmask = sb("cmask", [128, 128], F32)
        rmask = sb("rmask", [128, 512], F32)
        gng_t = sb("gng_t", [128, 128], F32)
        attn_t = sb("attn_t", [128, 1024], F32)
        st6 = [sb("st6_%d" % i, [128, 4, 6], F32) for i in range(8)]
        mvt = [sb("mv_%d" % i, [128, 2], F32) for i in range(8)]
        sdt = [sb("sd_%d" % i, [128, 1], F32) for i in range(8)]
        rst = [sb("rs_%d" % i, [128, 1], F32) for i in range(8)]
        nmt = [sb("nm_%d" % i, [128, 1], F32) for i in range(8)]

        sm4 = sb("sm4", [128, 16], F32)
        onb = sb("onb", [128, 256], BF16)
        ones4 = sb("ones4", [128, 4], F32)
        mhalf = sb("mhalf", [128, 4], F32)
        Sbf = [sb("Sbf%d" % i, [128, 256], BF16) for i in range(6)]
        ebt = sb("ebt", [128, 2, 32], F32)
        junk = sb("junk", [128, 128], F32)
        tmpy = [sb("tmpy%d" % i, [128, 256], F32) for i in range(2)]
        sb_dg0 = sb("dg0", [128, 128], F32)
        sb_dg1 = sb("dg1", [128, 128], F32)
        ps = [psum("ps%d" % i, [128, 512], F32) for i in range(7)]
        psT = psum("psT", [128, 1024], BF16)

        _an = [0]

        def alloc(stack, shape, dtype, side=None):
            _an[0] += 1
            kw = {"side": side} if side else {}
            t = stack.enter_context(nc.sbuf_tensor("sb_a%d" % _an[0], shape, dtype, **kw))
            return t[:]

        def carve(off, shape, dtype):
            n = 1
            for d_ in shape[1:]:
                n *= d_
            nbytes = n * (2 if dtype == BF16 else 4)
            assert off % 4 == 0 and off + nbytes <= ARENA_B, (off, nbytes)
            v = arena[:, off // 4:(off + nbytes) // 4]
            if dtype == BF16:
                v = v.bitcast(BF16)
            if len(shape) == 3:
                v = v.rearrange("p (a b) -> p a b", a=shape[1])
            elif len(shape) == 4:
                v = v.rearrange("p (a b c) -> p a b c", a=shape[1], b=shape[2])
            return v

        KB = 1024

        def cload(dst, src, key):
            P.dma("sp", consts_sem, lambda e, d=dst, s=src: e.dma_start(out=d, in_=s), writes=[key])

        def OP(eng, fn, reads=(), writes=()):
            return P.op(eng, fn, reads, writes)

        def MM(out, lhsT, rhs, start, stop, reads, writes):
            return P.op("pe", lambda e: e.matmul(out, lhsT, rhs, start=start, stop=stop), reads, writes)

        def TR(out, in_, idt, reads, writes):
            return P.op("pe", lambda e: e.transpose(out, in_, idt), reads, writes)

        def ACT(out, in_, func, reads, writes, scale=None, bias=None, accum=None):
            kw = {}
            if scale is not None:
                kw["scale"] = scale
            if bias is not None:
                kw["bias"] = bias
            if accum is not None:
                kw["accum_out"] = accum
            return P.op("act", lambda e: e.activation(out=out, in_=in_, func=func, **kw), reads, writes)

        def TT(out, in0, in1, op, reads, writes, eng="dve"):
            return P.op(eng, lambda e: e.tensor_tensor(out=out, in0=in0, in1=in1, op=op), reads, writes)

        def TS(out, in0, s1, s2, op0, op1, reads, writes, eng="dve"):
            if op1 is None:
                return P.op(eng, lambda e: e.tensor_scalar(out=out, in0=in0, scalar1=s1, scalar2=None, op0=op0), reads, writes)
            return P.op(eng, lambda e: e.tensor_scalar(out=out, in0=in0, scalar1=s1, scalar2=s2, op0=op0, op1=op1), reads, writes)

        def STT(out, in0, scalar, in1, op0, op1, reads, writes):
            return P.op("dve", lambda e: e.scalar_tensor_tensor(out=out, in0=in0, scalar=scalar, in1=in1, op0=op0, op1=op1), reads, writes)

        def COPY(eng, out, in_, reads, writes):
            if eng == "act":
                return P.op("act", lambda e: e.activation(out=out, in_=in_, func=AF.Copy), reads, writes)
            return P.op(eng, lambda e: e.tensor_copy(out=out, in_=in_), reads, writes)

        def DMA(eng, sem, out, in_, reads, writes):
            return P.dma(eng, sem, lambda e: e.dma_start(out=out, in_=in_), reads, writes)

        cload(ident_f[:], identd, "ident_f")
        cload(valid[:], validd, "valid")
        cload(cmask[:], cmaskd, "cmask")
        cload(rmask[:], rmaskd, "rmask")
        cload(gng_t[:], gng, "gng")
        cload(attn_t[:], attng, "attn_t")
        OP("pool", lambda e: e.memset(epst[:], EPS), writes=["epst"])
        OP("pool", lambda e: e.memset(ones_f[:], 1.0), writes=["ones_f"])
        OP("pool", lambda e: e.memset(mhalf[:], -0.5), writes=["mhalf"])

        def ring_load(pieces, nk=16):
            s = ring_n[0] % RS
            ring_n[0] += 1
            wtot = sum(p[1] for p in pieces)
            view = ring[s][:, 0:nk * wtot].rearrange("p (k n) -> p k n", k=nk)
            key = "ring%d" % s
            step = 4
            first = True
            for (d0, wd, W, s0) in pieces:
                Wv = W.rearrange("(k p) n -> p k n", p=128)
                for k0 in range(0, nk, step):
                    k1 = min(nk, k0 + step)
                    DMA("pool", ring_sem[s], view[:, k0:k1, d0:d0 + wd], Wv[:, k0:k1, s0:s0 + wd],
                        [], [key] if first else [])
                    first = False
            P.reg[key] = [P.ops["pool"][-1], []]
            return view, key

        def ringE_load(W, col0, width, nk):
            s = ringE_n[0] % 2
            ringE_n[0] += 1
            view = ringE[s][:, 0:nk * width].rearrange("p (k n) -> p k n", k=nk)
            key = "ringE%d" % s
            Wv = W.rearrange("(k p) n -> p k n", p=128)
            first = True
            for k0 in range(0, nk, 4):
                k1 = min(nk, k0 + 4)
                DMA("pool", ringE_sem[s], view[:, k0:k1, :], Wv[:, k0:k1, col0:col0 + width], [], [key] if first else [])
                first = False
            P.reg[key] = [P.ops["pool"][-1], []]
            return view, key

        def ln_stats(src, srckey, par):
            k = "ln%d" % par
            for q in range(4):
                OP("dve", lambda e, q=q: e.bn_stats(out=st6[par][:, q, :], in_=src[:, q * 512:(q + 1) * 512]),
                   [srckey], [k + "st%d" % q])
            OP("dve", lambda e: e.bn_aggr(out=mvt[par][:], in_=st6[par][:].rearrange("p a b -> p (a b)")),
               [k + "st%d" % q for q in range(4)], [k + "mv"])
            ACT(sdt[par][:], mvt[par][:, 1:2], AF.Sqrt, [k + "mv", "epst"], [k + "sd"], scale=1.0, bias=epst[:, 0:1])
            OP("dve", lambda e: e.reciprocal(out=rst[par][:], in_=sdt[par][:]), [k + "sd"], [k + "rs"])
            TS(nmt[par][:], mvt[par][:, 0:1], rst[par][:, 0:1], -1.0, ALU.mult, ALU.mult, [k + "mv", k + "rs"], [k + "nm"])
            return [k + "rs", k + "nm"]

        def ln_a(src, srckey, par):
            k = "ln%d" % par
            for q in range(4):
                OP("dve", lambda e, q=q: e.bn_stats(out=st6[par][:, q, :], in_=src[:, q * 512:(q + 1) * 512]),
                   [srckey], [k + "st%d" % q])
            OP("dve", lambda e: e.bn_aggr(out=mvt[par][:], in_=st6[par][:].rearrange("p a b -> p (a b)")),
               [k + "st%d" % q for q in range(4)], [k + "mv"])

        def ln_b(par):
            k = "ln%d" % par
            ACT(sdt[par][:], mvt[par][:, 1:2], AF.Sqrt, [k + "mv", "epst"], [k + "sd"], scale=1.0, bias=epst[:, 0:1])

        def ln_c(par):
            k = "ln%d" % par
            OP("dve", lambda e: e.reciprocal(out=rst[par][:], in_=sdt[par][:]), [k + "sd"], [k + "rs"])
            TS(nmt[par][:], mvt[par][:, 0:1], rst[par][:, 0:1], -1.0, ALU.mult, ALU.mult, [k + "mv", k + "rs"], [k + "nm"])
            return [k + "rs", k + "nm"]

        def bc_from_fm(dst, dstkey, col0):
            dg = [sb_dg0, sb_dg1]
            for c in range(16):
                b_ = c % 2
                TS(dg[b_][:], ident_f[:], modfm[:, col0 + c:col0 + c + 1], None, ALU.mult, None,
                   ["ident_f", "modfm"], ["dg%d" % b_])
                MM(ps[b_][:, 0:128], ones_f[:], dg[b_][:], True, True, ["ones_f", "dg%d" % b_], ["psb%d" % b_])
                COPY("act", dst[:, c * 128:(c + 1) * 128], ps[b_][:, 0:128], ["psb%d" % b_], [dstkey + ".%d" % c])

        c_f = sb("c_f", [128, 16], F32)
        c_act = sb("c_act", [128, 16], BF16)
        bada = sb("bada", [128, 96], F32)
        lbl = sb("lbl", [128, 16], F32)
        cload(c_f[:], cfm, "c_f")
        cload(bada[:], badafm, "bada")
        cload(lbl[:], lbfm, "lbl")
        P.barrier()
        OP("dve", lambda e: e.tensor_copy(out=ident_b[:], in_=ident_f[:]), ["ident_f"], ["ident_b"])
        ACT(c_act[:], c_f[:], AF.Silu, ["c_f"], ["c_act"])
        TT(lbl[:, 0:8], lbl[:, 0:8], lbl[:, 8:16], ALU.subtract, ["lbl"], ["lbl"])
        ACT(lbv[:], lbl[:, 0:8], AF.Sigmoid, ["lbl"], ["lbv"])
        TS(oml[:], lbv[:], -1.0, 1.0, ALU.mult, ALU.add, ["lbv"], ["oml"])
        psm = ps[6]

        def ada_block(nb, dst=None, dkey="psm", c0=0):
            view, key = ring_load([(0, 512, w_ada, nb * 512)])
            dst = psm if dst is None else dst
            for j in range(4):
                col = nb * 4 + j - c0
                for k in range(16):
                    MM(dst[:, col:col + 1], view[:, k, j * 128:(j + 1) * 128], c_act[:, k:k + 1], k == 0, k == 15,
                       [key, "c_act"], [dkey])
        for nb in range(8):
            ada_block(nb)
        TT(modfm[:, 0:32], psm[:, 0:32], bada[:, 0:32], ALU.add, ["psm", "bada"], ["modfm"])
        TS(sc1p[:], modfm[:, 16:32], 1.0, None, ALU.add, None, ["modfm"], ["sc1p"])
        P.barrier()

        if stop_after == "0":
            P.disabled = True
        es_hT = ExitStack()
        es_oT = ExitStack()
        hT = alloc(es_hT, [128, 16, 2048], BF16)
        oT = alloc(es_oT, [128, 16, TOK], BF16, side="right")
        GB = 96 * KB

        def ln_mod_transpose(src, srckey, par, dstT, dstkey, tok0, scp, shcol, do_ln=True, act_only=False):
            if do_ln:
                keys = ln_stats(src, srckey, par)
                ACT(src, src, AF.Identity, [srckey] + keys, [srckey], scale=rst[par][:, 0:1], bias=nmt[par][:, 0:1])
            for g4 in range(4):
                bank = ps[g4]
                bkey = "psA%d" % g4
                for c4 in range(4):
                    c = g4 * 4 + c4
                    TR(bank[:, c4 * 128:(c4 + 1) * 128], src[:, c * 128:(c + 1) * 128], ident_f[:],
                       [srckey, "ident_f"], [bkey])
                for c4 in range(4):
                    c = g4 * 4 + c4
                    o_ = dstT[:, c, tok0:tok0 + 128]
                    i_ = bank[:, c4 * 128:(c4 + 1) * 128]
                    if g4 % 2 == 0 or act_only:
                        ACT(o_, i_, AF.Identity, [bkey, "sc", "modfm"], [dstkey + ".%d" % c],
                            scale=scp[:, c:c + 1], bias=modfm[:, shcol + c:shcol + c + 1])
                    else:
                        TS(o_, i_, scp[:, c:c + 1], modfm[:, shcol + c:shcol + c + 1], ALU.mult, ALU.add,
                           [bkey, "sc", "modfm"], [dstkey + ".%d" % c])

        es_a = ExitStack()
        xb = [alloc(es_a, [128, 2048], F32) for i in range(4)]
        xsem = [dsem() for _ in range(4)]

        def a_s1(TT_):
            for T in TT_:
                b_ = T % 4
                DMA("sp", xsem[b_], xb[b_], xs[T * 128:(T + 1) * 128, :], [], ["xb%d" % b_])
            for T in TT_:
                ln_a(xb[T % 4], "xb%d" % (T % 4), T % 8)
            for T in TT_:
                ln_b(T % 8)
            for T in TT_:
                ln_c(T % 8)
            for T in TT_:
                b_, par = T % 4, T % 8
                k = "ln%d" % par
                ACT(xb[b_], xb[b_], AF.Identity, ["xb%d" % b_, k + "rs", k + "nm"], ["xb%d" % b_],
                    scale=rst[par][:, 0:1], bias=nmt[par][:, 0:1])

        def a_s2(TT_):
            for T in TT_:
                ln_mod_transpose(xb[T % 4], "xb%d" % (T % 4), 0, hT, "hT.%d" % T, T * 128, sc1p, 0, do_ln=False)

        for p in range(8):
            a_s1([2 * p, 2 * p + 1])
            if p >= 1:
                a_s2([2 * p - 2, 2 * p - 1])
        a_s2([14, 15])
        P.barrier()
        es_a.close()

        def hkeys(T0, T1, kc):
            return ["hT.%d.%d" % (T, kc) for T in range(T0, T1)]

        if stop_after == "A":
            P.disabled = True
        es_b1 = ExitStack()
        kT = alloc(es_b1, [128, 2, 1536], BF16)
        vaug = alloc(es_b1, [128, 12, 4, 65], BF16)
        bhi = alloc(es_b1, [128, 4, 640], BF16)
        blo = alloc(es_b1, [128, 4, 640], BF16)
        bstg2 = [alloc(es_b1, [128, 640], F32) for i in range(2)]
        bsem2 = [dsem(), dsem()]
        qTz = alloc(es_b1, [128, 4, 512], BF16)
        pT = [alloc(es_b1, [128, 4, 640], BF16) for i in range(2)]
        onba = [alloc(es_b1, [128, 256], BF16) for i in range(2)]
        o_sb = alloc(es_b1, [128, 4, 64], F32)
        o_sq = alloc(es_b1, [128, 4, 64], F32)
        bsem = dsem()
        OP("pool", lambda e: e.memset(ones4[:], 1.0), [], ["ones4"])
        for t in range(12):
            ACT(vaug[:, t, :, 64:65], ones4[:].unsqueeze(2), AF.Identity, ["ones4", "valid"], ["vaug1.%d" % t],
                scale=valid[:, t + 4:t + 5])
        OP("dve", lambda e: e.memset(qTz, 0.0), [], ["qTz.%d" % h for h in range(4)])
        for ga in range(4):
            kv, kvkey = ring_load([(0, 256, w_in, 1024 + ga * 256), (256, 256, w_in, 2048 + ga * 256)])
            qv, qkey = ring_load([(0, 256, w_in, ga * 256)])
            for h in range(4):
                bs, bk = bstg2[h % 2], "bstg%d" % (h % 2)
                DMA("sp", bsem2[h % 2], bs, biasd[ga * 4 + h], [], [bk])
                COPY("dve", bhi[:, h, :], bs, [bk], ["bhi.%d" % h])
                TT(blo[:, h, :], bs, bhi[:, h, :], ALU.subtract, [bk, "bhi.%d" % h], ["blo.%d" % h])
            nproj = [0]

            def proj_bank():
                i = nproj[0] % 2
                nproj[0] += 1
                return ps[i], "psP%d" % i
            for tg in range(1, 4):
                for pr in range(2):
                    bank, bkey = proj_bank()
                    for kc in range(16):
                        MM(bank[:, 0:512], kv[:, kc, pr * 128:(pr + 1) * 128], hT[:, kc, tg * 512:(tg + 1) * 512],
                           kc == 0, kc == 15, [kvkey] + hkeys(tg * 4, tg * 4 + 4, kc), [bkey])
                    COPY("act" if pr == 0 else "dve", kT[:, pr, (tg - 1) * 512:tg * 512], bank[:, 0:512],
                         [bkey], ["kT.%d.%d" % (pr, tg)])
            for T in range(4, 16):
                bank, bkey = proj_bank()
                for kc in range(16):
                    MM(bank[:, 0:256], hT[:, kc, T * 128:(T + 1) * 128], kv[:, kc, 256:512], kc == 0, kc == 15,
                       [kvkey, "hT.%d.%d" % (T, kc)], [bkey])
                ACT(vaug[:, T - 4, :, 0:64], bank[:, 0:256].rearrange("p (h d) -> p h d", h=4), AF.Identity,
                    [bkey, "valid"], ["vaug.%d" % (T - 4)], scale=valid[:, T:T + 1])
            def att_scores(T, ti):
                pb = T % 2
                for h in range(4):
                    pr, hf = h // 2, h % 2
                    sbank = ps[2 + h % 2]
                    skey = "psS%d" % (h % 2)
                    bbank = ps[4 if hf == 0 else 6]
                    bkey_ = "psB%d" % hf
                    MM(sbank[:, 0:512], ident_b[:], bhi[:, h, 0:512], True, False, ["ident_b", "bhi.%d" % h], [skey])
                    MM(sbank[:, 0:512], ident_b[:], blo[:, h, 0:512], False, False, ["ident_b", "blo.%d" % h], [skey])
                    for jb in range(4):
                        off = (T - 8 + jb) * 128
                        tgk = 1 + off // 512
                        MM(sbank[:, jb * 128:(jb + 1) * 128], kT[:, pr, off:off + 128], qTz[:, h, ti * 128:(ti + 1) * 128],
                           False, jb == 3, ["kT.%d.%d" % (pr, tgk), "qTz.%d" % h], [skey])
                    off = (T - 8 + 4) * 128
                    tgk = 1 + off // 512
                    bo = bbank[:, pr * 128:(pr + 1) * 128]
                    MM(bo, ident_b[:], bhi[:, h, 512:640], True, False, ["ident_b", "bhi.%d" % h], [bkey_])
                    MM(bo, ident_b[:], blo[:, h, 512:640], False, False, ["ident_b", "blo.%d" % h], [bkey_])
                    MM(bo, kT[:, pr, off:off + 128], qTz[:, h, ti * 128:(ti + 1) * 128], False, True,
                       ["kT.%d.%d" % (pr, tgk), "qTz.%d" % h], [bkey_])
                    ACT(pT[pb][:, h, 0:512], sbank[:, 0:512], AF.Exp, [skey], ["pT%d.%d" % (pb, h)])
                for hf in range(2):
                    ACT(pT[pb][:, hf:4:2, 512:640], ps[4 if hf == 0 else 6][:, 0:256].rearrange("p (h t) -> p h t", h=2),
                        AF.Exp, ["psB%d" % hf], ["pTb%d.%d" % (pb, hf), "pTb%d.%d" % (pb, hf + 2)])

            def att_pv(T):
                pb = T % 2
                for h in range(4):
                    for jb in range(5):
                        MM(ps[5][:, h * 65:(h + 1) * 65], pT[pb][:, h, jb * 128:(jb + 1) * 128], vaug[:, T - 8 + jb, h, :],
                           jb == 0, jb == 4,
                           ["pT%d.%d" % (pb, h), "pTb%d.%d" % (pb, h), "vaug.%d" % (T - 8 + jb), "vaug1.%d" % (T - 8 + jb)],
                           ["psO"])
                pso = ps[5][:, 0:260].rearrange("p (h d) -> p h d", h=4)
                OP("dve", lambda e, pso=pso: e.reciprocal(out=sm4[:, 0:4], in_=pso[:, :, 64]), ["psO"], ["sm4a"])
                TT(o_sb, pso[:, :, 0:64], sm4[:, 0:4].unsqueeze(2).to_broadcast([128, 4, 64]), ALU.mult,
                   ["psO", "sm4a"], ["o_sb"])
                TT(o_sq, o_sb, o_sb, ALU.mult, ["o_sb"], ["o_sq"])
                OP("dve", lambda e: e.tensor_reduce(out=sm4[:, 4:8], in_=o_sq, axis=AX.X, op=ALU.add), ["o_sq"], ["sm4b"])
                ACT(sm4[:, 8:12], sm4[:, 4:8], AF.Ln, ["sm4b", "epst"], ["sm4c"], scale=1.0 / 64, bias=epst[:, 0:1])
                ACT(sm4[:, 12:16], sm4[:, 8:12], AF.Exp, ["sm4c"], ["sm4d"], scale=-0.5)
                TT(o_sb, o_sb, sm4[:, 12:16].unsqueeze(2).to_broadcast([128, 4, 64]), ALU.mult, ["o_sb", "sm4d"], ["o_sb"])
                TT(onba[pb].rearrange("p (h d) -> p h d", h=4), o_sb,
                   attn_t[:, ga * 256:(ga + 1) * 256].rearrange("p (h d) -> p h d", h=4), ALU.mult,
                   ["o_sb", "attn_t"], ["onba%d" % pb])

            def att_tr(T):
                pb = T % 2
                for pr in range(2):
                    TR(psT[:, pr * 128:(pr + 1) * 128], onba[pb][:, pr * 128:(pr + 1) * 128], ident_b[:],
                       ["onba%d" % pb, "ident_b"], ["psT"])
                COPY("act", oT[:, ga * 2:ga * 2 + 2, (T - 8) * 128:(T - 7) * 128],
                     psT[:, 0:256].rearrange("p (a t) -> p a t", a=2), ["psT"], ["oT.%d.%d" % (ga, T - 8)])

            for tq in range(2):
                for pr in range(2):
                    bank, bkey = proj_bank()
                    for kc in range(16):
                        MM(bank[:, 0:512], qv[:, kc, pr * 128:(pr + 1) * 128],
                           hT[:, kc, 1024 + tq * 512:1024 + (tq + 1) * 512], kc == 0, kc == 15,
                           [qkey] + hkeys(8 + tq * 4, 12 + tq * 4, kc), [bkey])
                    for hf in range(2):
                        ACT(qTz[hf * 64:(hf + 1) * 64, 2 * pr + hf, :], bank[hf * 64:(hf + 1) * 64, 0:512], AF.Copy, [bkey],
                            ["qTz.%d" % (2 * pr + hf)], scale=0.125)
                for ti in range(4):
                    T = 8 + tq * 4 + ti
                    att_scores(T, ti)
                    if (T - 8) in (1, 4, 6):
                        ada_block(8 + ga * 3 + {1: 0, 4: 1, 6: 2}[T - 8], dst=ps[5][:, 384:448], dkey="psO", c0=32)
                    if T - 1 >= 8:
                        att_pv(T - 1)
                    if T - 2 >= 8:
                        att_tr(T - 2)
            att_pv(15)
            att_tr(14)
            att_tr(15)
        TT(modfm[:, 32:80], ps[5][:, 384:432], bada[:, 32:80], ALU.add, ["psO", "bada"], ["modfmB"])
        TS(sc2p[:], modfm[:, 64:80], 1.0, None, ALU.add, None, ["modfmB"], ["sc2p"])
        P.barrier()

        if stop_after == "B1":
            P.disabled = True
        es_b1.close()
        es_b2 = ExitStack()
        nkTm = alloc(es_b2, [128, 2, 1024], BF16)
        qtT = alloc(es_b2, [128, 2, 1024], BF16)
        nktok = alloc(es_b2, [128, 16, 256], BF16)
        w1 = [alloc(es_b2, [128, 512], F32) for i in range(2)]
        w2 = [alloc(es_b2, [128, 512], F32) for i in range(2)]
        w3 = [alloc(es_b2, [128, 512], F32) for i in range(2)]
        nkh = [alloc(es_b2, [128, 512], BF16) for i in range(2)]
        isb = [alloc(es_b2, [128, 256], BF16) for i in range(3)]
        sgt = [alloc(es_b2, [128, 256], F32) for i in range(3)]
        atsb = [alloc(es_b2, [128, 256], BF16) for i in range(2)]
        onb2 = [alloc(es_b2, [128, 256], BF16) for i in range(2)]
        Sst2 = [alloc(es_b2, [128, 256], F32) for i in range(2)]
        tmpS2 = [alloc(es_b2, [128, 256], F32) for i in range(2)]
        onf = alloc(es_b2, [128, 256], F32)
        def rec_loads(g):
            a = ring_load([(0, 256, w_in, 4096 + g * 256), (256, 256, w_in, 3072 + g * 256)])
            b = ring_load([(0, 256, w_in, 6144 + g * 256), (256, 256, w_in, 5120 + g * 256)])
            return a, b
        nxt_loads = rec_loads(0)
        for gr in range(4):
            (fq, fqkey), (gi, gikey) = nxt_loads
            npj = [0]

            def fbank():
                i = npj[0] % 2
                npj[0] += 1
                return ps[i], "psF%d" % i
            for tg in range(4):
                main = tg >= 2
                HH = range(2)
                tsl = slice(tg * 512, (tg + 1) * 512)
                nkd = [nkTm[:, hh, (tg - 2) * 512:(tg - 1) * 512] if main else nkh[hh] for hh in HH]
                nkk = ["nkTm.%d.%d" % (hh, tg) if main else "nkh%d" % hh for hh in HH]
                for hh in HH:
                    for kc in range(16):
                        MM(ps[hh][:, 0:512], fq[:, kc, hh * 128:(hh + 1) * 128], hT[:, kc, tsl],
                           kc == 0, kc == 15, [fqkey] + hkeys(tg * 4, tg * 4 + 4, kc), ["psF%d" % hh])
                for hh in HH:
                    ACT(w1[hh], ps[hh][:, 0:512], AF.Sigmoid, ["psF%d" % hh], ["w1.%d" % hh])
                for hh in HH:
                    hd = gr * 2 + hh
                    TS(w1[hh], w1[hh], oml[:, hd:hd + 1], lbv[:, hd:hd + 1], ALU.mult, ALU.add,
                       ["w1.%d" % hh, "oml", "lbv"], ["w1.%d" % hh])
                for hh in HH:
                    ACT(w2[hh], w1[hh], AF.Ln, ["w1.%d" % hh], ["w2.%d" % hh])
                for hh in HH:
                    OP("dve", lambda e, hh=hh: e.tensor_tensor_scan(out=w3[hh], data0=rmask[:], data1=w2[hh], initial=0.0,
                                                                   op0=ALU.mult, op1=ALU.add),
                       ["rmask", "w2.%d" % hh], ["w3.%d" % hh])
                for hh in HH:
                    ACT(w2[hh], w3[hh], AF.Exp, ["w3.%d" % hh], ["w2.%d" % hh], scale=-1.0)
                for hh in HH:
                    STT(nkd[hh], w1[hh], 1.0, w2[hh], ALU.subtract, ALU.mult, ["w1.%d" % hh, "w2.%d" % hh], [nkk[hh]])
                for hh in HH:
                    ACT(ebt[:, hh, tg * 8:(tg + 1) * 8], w3[hh][:, 63:512:64], AF.Exp, ["w3.%d" % hh],
                        ["ebt.%d.%d" % (hh, tg)])
                for hh in HH:
                    for j in range(4):
                        TR(psT[:, hh * 512 + j * 128:hh * 512 + (j + 1) * 128], nkd[hh][:, j * 128:(j + 1) * 128], ident_b[:],
                           [nkk[hh], "ident_b"], ["psT"])
                for hh in HH:
                    COPY("dve", nktok[:, tg * 4:(tg + 1) * 4, hh * 128:(hh + 1) * 128],
                         psT[:, hh * 512:(hh + 1) * 512].rearrange("p (j d) -> p j d", j=4), ["psT"],
                         ["nktok.%d.%d" % (tg, hh)])
                if main:
                    for hh in HH:
                        for kc in range(16):
                            MM(ps[hh][:, 0:512], fq[:, kc, 256 + hh * 128:256 + (hh + 1) * 128], hT[:, kc, tsl],
                               kc == 0, kc == 15, [fqkey] + hkeys(tg * 4, tg * 4 + 4, kc), ["psF%d" % hh])
                    for hh in HH:
                        ACT(w1[hh], ps[hh][:, 0:512], AF.Silu, ["psF%d" % hh], ["w1.%d" % hh])
                    for hh in HH:
                        ACT(w2[hh], w3[hh], AF.Exp, ["w3.%d" % hh], ["w2.%d" % hh])
                    for hh in HH:
                        TT(qtT[:, hh, (tg - 2) * 512:(tg - 1) * 512], w1[hh], w2[hh], ALU.mult,
                           ["w1.%d" % hh, "w2.%d" % hh], ["qtT.%d.%d" % (hh, tg)])
            if gr + 1 < 4:
                nxt_loads = rec_loads(gr + 1)
            OP("dve", lambda e: e.memset(Sst2[0], 0.0), [], ["S0.0", "S0.1"])
            OP("dve", lambda e: e.memset(Sbf[0][:], 0.0), [], ["Sbf0.0", "Sbf0.1"])

            def part_at(T):
                tgm = T // 4
                c0 = (T % 2) * 256
                for hh in range(2):
                    MM(ps[5][:, c0 + hh * 128:c0 + (hh + 1) * 128], nkTm[:, hh, (T - 8) * 128:(T - 7) * 128],
                       qtT[:, hh, (T - 8) * 128:(T - 7) * 128], True, True,
                       ["nkTm.%d.%d" % (hh, tgm), "qtT.%d.%d" % (hh, tgm)], ["psAT"])

            def part_mask(T):
                c0 = (T % 2) * 256
                TT(atsb[T % 2].rearrange("p (h t) -> p h t", h=2), ps[5][:, c0:c0 + 256].rearrange("p (h t) -> p h t", h=2),
                   cmask[:].unsqueeze(1).to_broadcast([128, 2, 128]), ALU.mult, ["psAT", "cmask"], ["atsb%d" % (T % 2)])

            def part_o(T):
                ib = isb[T % 3]
                ibk = "isb%d" % (T % 3)
                tgm = T // 4
                at = atsb[T % 2]
                ob_ = onb2[T % 2]
                p_ = T % 2
                pob = ps[6] if p_ == 0 else ps[0]
                pkey = "psOB" if p_ == 0 else "psF0"
                a0, t0_, r0 = 2 * p_, 4 + 2 * p_, 8 + 2 * p_
                for hh in range(2):
                    MM(pob[:, hh * 128:(hh + 1) * 128], at[:, hh * 128:(hh + 1) * 128], ib[:, hh * 128:(hh + 1) * 128],
                       True, False, ["atsb%d" % (T % 2), ibk], [pkey])
                    for c2 in range(2):
                        ci = 2 * T + c2
                        t0 = (T - 8) * 128 + c2 * 64
                        MM(pob[c2 * 64:(c2 + 1) * 64, hh * 128:(hh + 1) * 128], qtT[:, hh, t0:t0 + 64],
                           Sbf[ci % 6][:, hh * 128:(hh + 1) * 128], False, True,
                           ["qtT.%d.%d" % (hh, tgm), "Sbf%d.%d" % (ci % 6, hh)], [pkey])
                for hh in range(2):
                    ACT(junk[:], pob[:, hh * 128:(hh + 1) * 128], AF.Square, [pkey], ["junk", "sm4r%d.%d" % (p_, hh)],
                        accum=sm4[:, a0 + hh:a0 + hh + 1])
                TS(sm4[:, t0_:t0_ + 2], sm4[:, a0:a0 + 2], 1.0 / 128, EPS, ALU.mult, ALU.add,
                   ["sm4r%d.0" % p_, "sm4r%d.1" % p_], ["sm4s%d" % p_], eng="pool")
                TT(sm4[:, r0:r0 + 2], sm4[:, t0_:t0_ + 2], mhalf[:, 0:2], ALU.pow, ["sm4s%d" % p_, "mhalf"], ["sm4t%d" % p_],
                   eng="pool")
                for hh in range(2):
                    STT(onf[:, hh * 128:(hh + 1) * 128], pob[:, hh * 128:(hh + 1) * 128], sm4[:, r0 + hh:r0 + hh + 1], gng_t[:],
                        ALU.mult, ALU.mult, [pkey, "sm4t%d" % p_, "gng"], ["onf.%d" % hh])
                TT(ob_, onf, sgt[T % 3], ALU.mult, ["onf.0", "onf.1", "sgt%d" % (T % 3)], ["onb2_%d" % (T % 2)])

            def part_tr(T):
                ob_ = onb2[T % 2]
                for hh in range(2):
                    TR(psT[:, hh * 128:(hh + 1) * 128], ob_[:, hh * 128:(hh + 1) * 128], ident_b[:],
                       ["onb2_%d" % (T % 2), "ident_b"], ["psT"])
                COPY("act", oT[:, 8 + gr * 2:10 + gr * 2, (T - 8) * 128:(T - 7) * 128],
                     psT[:, 0:256].rearrange("p (a t) -> p a t", a=2), ["psT"], ["oT.%d.%d" % (8 + gr, T - 8)])

            def gi_proj(T):
                main = T >= 8
                bank = ps[2 + T % 2]
                bkey = "psGI%d" % (T % 2)
                n = 512 if main else 256
                rhs_lo = 0 if main else 256
                for kc in range(16):
                    MM(bank[:, 0:n], hT[:, kc, T * 128:(T + 1) * 128], gi[:, kc, rhs_lo:512], kc == 0, kc == 15,
                       [gikey, "hT.%d.%d" % (T, kc)], [bkey])
                if main:
                    part_at(T)
                ib = isb[T % 3]
                ibk = "isb%d" % (T % 3)
                ACT(ib, bank[:, n - 256:n], AF.Identity, [bkey, "valid"], [ibk], scale=valid[:, T:T + 1])
                if main:
                    ACT(sgt[T % 3], bank[:, 0:256], AF.Silu, [bkey], ["sgt%d" % (T % 3)])

            def gi_scan(T):
                ib = isb[T % 3]
                ibk = "isb%d" % (T % 3)
                for c2 in range(2):
                    ci = 2 * T + c2
                    dsb = ps[4] if c2 == 0 else ps[1]
                    dsl = dsb[:, 0:256]
                    for hh in range(2):
                        MM(dsb[:, hh * 128:(hh + 1) * 128],
                           nktok[c2 * 64:(c2 + 1) * 64, T, hh * 128:(hh + 1) * 128],
                           ib[c2 * 64:(c2 + 1) * 64, hh * 128:(hh + 1) * 128], True, True,
                           ["nktok.%d.%d" % (T // 4, hh), ibk], ["psDS0" if c2 == 0 else "psF1"])
                    tb = tmpS2[ci % 2]
                    tbk = "tmpS%d" % (ci % 2)
                    TT(tb.rearrange("p (h e) -> p h e", h=2), dsl.rearrange("p (h e) -> p h e", h=2),
                       ebt[:, :, ci:ci + 1].to_broadcast([128, 2, 128]), ALU.mult,
                       ["psDS0" if c2 == 0 else "psF1", "ebt.0.%d" % (T // 4), "ebt.1.%d" % (T // 4)], [tbk])
                    nxt = (ci + 1) % 6
                    for hh in range(2):
                        ebap = ebt[:, hh, ci:ci + 1]
                        Sin, Sout = Sst2[ci % 2], Sst2[(ci + 1) % 2]
                        STT(Sout[:, hh * 128:(hh + 1) * 128], Sin[:, hh * 128:(hh + 1) * 128], ebap,
                            tb[:, hh * 128:(hh + 1) * 128], ALU.mult, ALU.subtract,
                            ["S%d.%d" % (ci % 2, hh), tbk, "ebt.%d.%d" % (hh, T // 4)], ["S%d.%d" % ((ci + 1) % 2, hh)])
                        TS(Sbf[nxt][:, hh * 128:(hh + 1) * 128], Sout[:, hh * 128:(hh + 1) * 128], 1.0, 1.0,
                           ALU.mult, ALU.mult, ["S%d.%d" % ((ci + 1) % 2, hh)], ["Sbf%d.%d" % (nxt, hh)], eng="pool")

            gi_proj(0)
            for T in range(NT):
                if T + 1 < NT:
                    gi_proj(T + 1)
                gi_scan(T)
                if T >= 8:
                    part_mask(T)
                if T - 1 >= 8:
                    part_o(T - 1)
                if T - 2 >= 8:
                    part_tr(T - 2)
            part_o(NT - 1)
            part_tr(NT - 2)
            part_tr(NT - 1)
        P.barrier()
        if stop_after == "B2":
            P.disabled = True
        if debug:
            dsd = dsem()
            if "hT8" in dbg:
                DMA("pool", dsd, dbg["hT8"], hT[:, :, 1024:1152], [], [])
            if "oT" in dbg:
                DMA("pool", dsd, dbg["oT"], oT, [], [])
            P.barrier()
        es_b2.close()
        es_hT.close()
        es_x1 = ExitStack()
        es_c1 = ExitStack()
        x1p = alloc(es_x1, [128, 8, 2048], F32)
        g1bc = alloc(es_c1, [128, 2048], F32)
        xp = [alloc(es_c1, [128, 512], F32) for i in range(2)]
        xpsem = [dsem(), dsem()]
        bc_from_fm(g1bc, "g1bc", 32)
        n_ = 0
        for nb in range(4):
            wv, wkey = ring_load([(0, 512, w_o, nb * 512)])
            for T in range(8):
                b_ = n_ % 2
                n_ += 1
                DMA("sp", xpsem[b_], xp[b_], xs[1024 + T * 128:1024 + (T + 1) * 128, nb * 512:(nb + 1) * 512],
                    [], ["xp%d" % b_])
                bank, bkey = ps[2 + b_], "psC%d" % b_
                for kc in range(16):
                    MM(bank[:, 0:512], oT[:, kc, T * 128:(T + 1) * 128], wv[:, kc, :], kc == 0, kc == 15, [wkey], [bkey])
                dst = x1p[:, T, nb * 512:(nb + 1) * 512]
                dk = "x1p.%d.%d" % (T, nb)
                TT(dst, bank[:, 0:512], g1bc[:, nb * 512:(nb + 1) * 512], ALU.mult,
                   [bkey] + ["g1bc.%d" % c for c in range(nb * 4, nb * 4 + 4)], [dk])
                STT(dst, xp[b_], ALPHA, dst, ALU.mult, ALU.add, ["xp%d" % b_, dk], [dk])
        P.barrier()

        if stop_after == "C1":
            P.disabled = True
        es_c1.close()
        es_oT.close()
        es_c2 = ExitStack()
        es_h2 = ExitStack()
        ln1g_t = alloc(es_c2, [128, 2048], F32)
        ln1b_t = alloc(es_c2, [128, 2048], F32)
        xn2 = [alloc(es_c2, [128, 2048], F32) for i in range(2)]
        h2T = alloc(es_h2, [128, 16, TOK], BF16, side="right")
        lsem = dsem()
        DMA("sp", lsem, ln1g_t, ln1g, [], ["lnp"])
        DMA("sp", lsem, ln1b_t, ln1b, [], [])
        P.reg["lnp"] = [P.ops["sp"][-1], []]
        stsem = [dsem(), dsem()]
        for nb in range(20, 24):
            ada_block(nb)

        def ln_affine(src, skey, par, g_t, b_t, add_eng="dve"):
            keys = ln_stats(src, skey, par)
            ACT(src, src, AF.Identity, [skey] + keys, [skey], scale=rst[par][:, 0:1], bias=nmt[par][:, 0:1])
            TT(src, src, g_t, ALU.mult, [skey, "lnp"], [skey])
            TT(src, src, b_t, ALU.add, [skey, "lnp"], [skey], eng=add_eng)

        X1 = [(x1p[:, T, :], "x1.%d" % T) for T in range(8)]
        for T in range(8):
            ln_a(X1[T][0], X1[T][1], T)
        for T in range(8):
            ln_b(T)
        for T in range(8):
            ln_c(T)
        for T in range(8):
            src, skey = X1[T]
            k = "ln%d" % T
            ACT(src, src, AF.Identity, [skey, k + "rs", k + "nm"], [skey], scale=rst[T][:, 0:1], bias=nmt[T][:, 0:1])
            TT(src, src, ln1g_t, ALU.mult, [skey, "lnp"], [skey])
            TT(src, src, ln1b_t, ALU.add, [skey, "lnp"], [skey])
            DMA("sp", stsem[T % 2], zscr[T * 128:(T + 1) * 128, :], src, [skey], ["zs.%d" % T])
        for T in range(8):
            ln_a(X1[T][0], X1[T][1], T)
        for T in range(8):
            ln_b(T)
        for T in range(8):
            ln_c(T)
        for T in range(8):
            src, skey = X1[T]
            par = T % 2
            k = "ln%d" % T
            ACT(xn2[par], src, AF.Identity, [skey, k + "rs", k + "nm"], ["xn2_%d" % par], scale=rst[T][:, 0:1],
                bias=nmt[T][:, 0:1])
            ln_mod_transpose(xn2[par], "xn2_%d" % par, par, h2T, "h2T.%d" % T, T * 128, sc2p, 48, do_ln=False)
        TT(modfm[:, 80:96], psm[:, 80:96], bada[:, 80:96], ALU.add, ["psm", "bada"], ["modfmD"])
        P.barrier()

        if stop_after == "C2":
            P.disabled = True
        es_c2.close()
        es_x1.close()
        es_act = ExitStack()
        es_d = ExitStack()
        actT = alloc(es_act, [128, 44, TOK], BF16)
        sgb = [alloc(es_d, [128, 512], F32) for i in range(2)]
        n_ = 0
        for j2 in range(22):
            wv, wkey = ring_load([(0, 128, w_f1, (2 * j2) * 128), (128, 128, w_f1, DFF + (2 * j2) * 128),
                                  (256, 128, w_f1, (2 * j2 + 1) * 128), (384, 128, w_f1, DFF + (2 * j2 + 1) * 128)])
            for jj in range(2):
                j = 2 * j2 + jj
                for th in range(2):
                    bi = n_ % 2
                    n_ += 1
                    bG, kG = ps[2 * bi], "psG%d" % bi
                    bU, kU = ps[2 * bi + 1], "psU%d" % bi
                    for kc in range(16):
                        MM(bG[:, 0:512], wv[:, kc, jj * 256:jj * 256 + 128], h2T[:, kc, th * 512:(th + 1) * 512],
                           kc == 0, kc == 15, [wkey], [kG])
                    for kc in range(16):
                        MM(bU[:, 0:512], wv[:, kc, jj * 256 + 128:jj * 256 + 256], h2T[:, kc, th * 512:(th + 1) * 512],
                           kc == 0, kc == 15, [wkey], [kU])
                    ACT(sgb[bi], bG[:, 0:512], AF.Silu, [kG], ["sgb%d" % bi])
                    TT(actT[:, j, th * 512:(th + 1) * 512], sgb[bi], bU[:, 0:512], ALU.mult, ["sgb%d" % bi, kU],
                       ["actT.%d.%d" % (j, th)])
        P.barrier(pool=True)

        if stop_after == "D":
            P.disabled = True
        es_d.close()
        es_h2.close()
        es_e = ExitStack()
        g2bc = alloc(es_e, [128, 2048], F32, side="right")
        ln2g_t = alloc(es_e, [128, 2048], F32, side="right")
        ln2b_t = alloc(es_e, [128, 2048], F32, side="right")
        es_e1 = ExitStack()
        zb = [alloc(es_e1, [128, 8, 256], F32, side="right") for i in range(2)]
        DMA("sp", lsem, ln2g_t, ln2g, [], ["lnp"])
        DMA("sp", lsem, ln2b_t, ln2b, [], [])
        P.reg["lnp"] = [P.ops["sp"][-1], []]
        bc_from_fm(g2bc, "g2bc", 80)
        zv = zscr.rearrange("(t p) n -> p t n", p=128)
        zlsem = [dsem(), dsem()]
        zssem = [dsem(), dsem()]
        for nbk in range(8):
            wv, wkey = ringE_load(w_f2, nbk * 256, 256, 44)
            b_ = nbk % 2
            zkeys = ["zb%d.%d" % (b_, T) for T in range(8)]
            DMA("sp", zlsem[b_], zb[b_], zv[:, :, nbk * 256:(nbk + 1) * 256], [], zkeys)
            for T in range(8):
                bank, bkey = ps[2 + T % 2], "psY%d" % (T % 2)
                for kc in range(44):
                    MM(bank[:, 0:256], actT[:, kc, T * 128:(T + 1) * 128], wv[:, kc, :], kc == 0, kc == 43, [wkey], [bkey])
                TT(tmpy[T % 2][:], bank[:, 0:256], g2bc[:, nbk * 256:(nbk + 1) * 256], ALU.mult,
                   [bkey, "g2bc.%d" % (2 * nbk), "g2bc.%d" % (2 * nbk + 1)], ["tmpy%d" % (T % 2)])
                STT(zb[b_][:, T, :], zb[b_][:, T, :], ALPHA, tmpy[T % 2][:], ALU.mult, ALU.add,
                    ["tmpy%d" % (T % 2), zkeys[T]], [zkeys[T]])
            DMA("sp", zssem[b_], zv[:, :, nbk * 256:(nbk + 1) * 256], zb[b_], zkeys, [])
        P.barrier()
        es_e1.close()
        es_act.close()
        es_e2 = ExitStack()
        ob = [alloc(es_e2, [128, 2048], F32) for i in range(8)]
        olsem = [dsem() for _ in range(8)]
        ossem = [dsem() for _ in range(8)]
        for T in range(8):
            DMA("sp", olsem[T], ob[T], zscr[T * 128:(T + 1) * 128, :], [], ["ob%d" % T])
        for T in range(8):
            ln_a(ob[T], "ob%d" % T, T)
        for T in range(8):
            ln_b(T)
        for T in range(8):
            ln_c(T)
        for T in range(8):
            okey = "ob%d" % T
            k = "ln%d" % T
            ACT(ob[T], ob[T], AF.Identity, [okey, k + "rs", k + "nm"], [okey], scale=rst[T][:, 0:1], bias=nmt[T][:, 0:1])
            TT(ob[T], ob[T], ln2g_t, ALU.mult, [okey, "lnp"], [okey])
            TT(ob[T], ob[T], ln2b_t, ALU.add, [okey, "lnp"], [okey], eng=("pool" if T == 7 else "dve"))
            DMA("sp", ossem[T], outd[T * 128:(T + 1) * 128, :], ob[T], [okey], [])
        if stop_after is not None:
            P.disabled = False
            fsem = dsem()
            DMA("sp", fsem, outd[0:128, 0:128], ident_f[:], [], [])
        P.barrier()
        OP("sp", lambda e: e.nop())
        OP("act", lambda e: e.nop())

        with nc.Block() as block:
            P.finalize(block, sems)
        es_e2.close()
        es_e.close()
    return nc


def _host_inputs(inputs):
    x = np.asarray(inputs["x"], np.float32)
    c = np.asarray(inputs["c"], np.float32)
    rel = np.asarray(inputs["rel_bias"], np.float32)[0]
    jj = np.arange(128)[:, None]
    u = np.arange(640)[None, :]
    jb = u // 128
    tt = u % 128
    relidx = 512 + tt - jb * 128 - jj
    idx = np.clip(relidx, -256, 256) + 256
    kc = (jb * 128 + jj) // 64
    qc = 8 + tt // 64
    ok = (qc - kc >= 0) & (qc - kc <= 8)
    rel_ext = np.concatenate([rel, np.full((16, 1), NEG, np.float32)], axis=1)
    idx = np.where(ok, idx, 513)
    biastab = np.ascontiguousarray(rel_ext[:, idx])
    s = np.arange(128)[:, None]
    t = np.arange(128)[None, :]
    cmask = np.where((s // 64 == t // 64) & (s <= t), -1.0, 0.0).astype(np.float32)
    rmask = np.tile((np.arange(512) % 64 != 0).astype(np.float32)[None, :], (128, 1))
    ident = np.eye(128, dtype=np.float32)
    bc = lambda v, n: np.ascontiguousarray(np.broadcast_to(np.asarray(v, np.float32).reshape(1, n), (128, n)))
    lb = np.asarray(inputs["lb_logits"], np.float32)
    lbfm = np.concatenate([lb[0].reshape(8, 128).T, lb[1].reshape(8, 128).T], axis=1)
    shared = {
        "badafm": np.ascontiguousarray(np.asarray(inputs["b_ada"], np.float32)[0].reshape(96, 128).T),
        "lbfm": np.ascontiguousarray(lbfm),
        "w_ada": np.ascontiguousarray(np.asarray(inputs["w_ada"], np.float32)[0]),
        "w_in": np.ascontiguousarray(np.asarray(inputs["w_in"], np.float32)[0]),
        "w_o": np.ascontiguousarray(np.asarray(inputs["w_o"], np.float32)[0]),
        "w_f1": np.ascontiguousarray(np.asarray(inputs["w_ffn_in"], np.float32)[0]),
        "w_f2": np.ascontiguousarray(np.asarray(inputs["w_ffn_out"], np.float32)[0]),
        "biastab": biastab,
        "attng": bc(np.asarray(inputs["attn_norm_g"])[0], 1024),
        "gng": bc(np.asarray(inputs["gnorm_g"])[0], 128),
        "ln1g": bc(np.asarray(inputs["ln1_g"])[0], D),
        "ln1b": bc(np.asarray(inputs["ln1_b"])[0], D),
        "ln2g": bc(np.asarray(inputs["ln2_g"])[0], D),
        "ln2b": bc(np.asarray(inputs["ln2_b"])[0], D),
        "cmask": cmask, "rmask": rmask, "ident": ident,
    }
    maps = []
    for core in range(8):
        b, half = core // 2, core % 2
        if half == 0:
            xe = np.concatenate([np.zeros((1024, D), np.float32), x[b, 0:1024]], axis=0)
            v = np.concatenate([np.zeros((128, 8), np.float32), np.ones((128, 8), np.float32)], axis=1)
        else:
            xe = x[b]
            v = np.ones((128, NT), np.float32)
        m = dict(shared)
        m["xs"] = np.ascontiguousarray(xe)
        m["valid"] = np.ascontiguousarray(v)
        m["cfm"] = np.ascontiguousarray(c[b].reshape(16, 128).T)
        maps.append(m)
    return maps


def kernel(**inputs):
    maps = _host_inputs(inputs)
    nc = build_program()
    res = run_bass_kernel_spmd(nc, maps, core_ids=list(range(8)))
    out = np.zeros((NB, SEQ, D), np.float32)
    for core in range(8):
        b, half = core // 2, core % 2
        out[b, half * 1024:(half + 1) * 1024] = res.results[core]["out"]
    return out
```

```python
import numpy as np
import concourse.bass as bass
import concourse.mybir as mybir
from concourse.bass_utils import run_bass_kernel_spmd

F32 = mybir.dt.float32
BF16 = mybir.dt.bfloat16
AF = mybir.ActivationFunctionType
ALU = mybir.AluOpType
AX = mybir.AxisListType

D = 2048
SEQ = 2048
NB = 4
TOK = 1024
NT = 16
DFF = 5632
EPS = 1e-5
ALPHA = 2.0 ** 0.25
NEG = -30000.0

ENGS = ("pe", "act", "dve", "pool", "sp")


class _Op:
    __slots__ = ("eng", "fn", "deps", "needed", "val", "dma", "dval")

    def __init__(self, eng, fn, deps, dma=None):
        self.eng = eng
        self.fn = fn
        self.deps = deps
        self.needed = False
        self.val = None
        self.dma = dma
        self.dval = None


class _DmaSem:
    def __init__(self, handle):
        self.handle = handle
        self.count = 0


class Prog:
    def __init__(self, nc):
        self.nc = nc
        self.ops = {e: [] for e in ENGS}
        self.reg = {}
        self.bar = {e: [] for e in ENGS}
        self.disabled = False

    def _deps_for(self, eng, reads, writes):
        deps = []
        for r in reads:
            st = self.reg.get(r)
            if st and st[0] is not None:
                deps.append(st[0])
        for w in writes:
            st = self.reg.get(w)
            if st:
                if st[0] is not None:
                    deps.append(st[0])
                deps.extend(st[1])
        return deps

    def _commit(self, op, reads, writes):
        for r in reads:
            st = self.reg.setdefault(r, [None, []])
            st[1].append(op)
        for w in writes:
            self.reg[w] = [op, []]

    def op(self, eng, fn, reads=(), writes=()):
        if self.disabled:
            return None
        deps = self._deps_for(eng, reads, writes)
        if self.bar[eng]:
            deps.extend(self.bar[eng])
            self.bar[eng] = []
        o = _Op(eng, fn, deps)
        self.ops[eng].append(o)
        self._commit(o, reads, writes)
        return o

    def dma(self, eng, sem, fn, reads=(), writes=()):
        if self.disabled:
            return None
        deps = self._deps_for(eng, reads, writes)
        if self.bar[eng]:
            deps.extend(self.bar[eng])
            self.bar[eng] = []
        o = _Op(eng, fn, deps, dma=sem)
        sem.count += 16
        o.dval = sem.count
        self.ops[eng].append(o)
        self._commit(o, reads, writes)
        return o

    def barrier(self, pool=False):
        last = []
        for e in ENGS:
            if self.ops[e]:
                for o in reversed(self.ops[e]):
                    if o.dma is None:
                        last.append(o)
                        break
        dl = {}
        for e in ENGS:
            for o in self.ops[e]:
                if o.dma is not None:
                    dl[id(o.dma)] = o
        last.extend(dl.values())
        for e in ENGS:
            if e == "pool" and not pool:
                continue
            self.bar[e] = list(last)
        if pool:
            self.reg = {}
        else:
            self.reg = {k: v for k, v in self.reg.items() if k.startswith("ring")}

    def finalize(self, block, sems):
        for e in ENGS:
            for o in self.ops[e]:
                for d in o.deps:
                    if d.dma is None:
                        if d.eng == "pe" and o.eng == "pe" and o.dma is None:
                            continue
                        d.needed = True
        for e in ENGS:
            c = 0
            for o in self.ops[e]:
                if o.dma is None and o.needed:
                    c += 1
                    o.val = c
        prog = self

        def emit(engname, engine):
            have = {}
            for o in prog.ops[engname]:
                for d in o.deps:
                    if d.dma is not None:
                        key = ("d", id(d.dma))
                        v = d.dval
                        h = d.dma.handle
                    else:
                        if d.eng == "pe" and engname == "pe" and o.dma is None:
                            continue
                        key = ("e", d.eng)
                        v = d.val
                        h = sems[d.eng]
                    if have.get(key, 0) >= v:
                        continue
                    have[key] = v
                    engine.wait_ge(h, v)
                ins = o.fn(engine)
                if o.dma is not None:
                    ins.then_inc(o.dma.handle, 16)
                elif o.needed:
                    ins.then_inc(sems[engname], 1)

        @block.tensor
        def _(eng):
            emit("pe", eng)

        @block.scalar
        def _(eng):
            emit("act", eng)

        @block.vector
        def _(eng):
            emit("dve", eng)

        @block.gpsimd
        def _(eng):
            emit("pool", eng)

        @block.sync
        def _(eng):
            emit("sp", eng)


def build_program(debug=None, stop_after=None):
    from contextlib import ExitStack
    nc = bass.Bass("TRN2", target_bir_lowering=False)
    dt = nc.dram_tensor
    xs = dt("xs", [SEQ, D], F32, kind="ExternalInput").ap()
    validd = dt("valid", [128, NT], F32, kind="ExternalInput").ap()
    cfm = dt("cfm", [128, 16], F32, kind="ExternalInput").ap()
    badafm = dt("badafm", [128, 96], F32, kind="ExternalInput").ap()
    lbfm = dt("lbfm", [128, 16], F32, kind="ExternalInput").ap()
    w_ada = dt("w_ada", [D, 6 * D], F32, kind="ExternalInput").ap()
    w_in = dt("w_in", [D, 7168], F32, kind="ExternalInput").ap()
    w_o = dt("w_o", [D, D], F32, kind="ExternalInput").ap()
    w_f1 = dt("w_f1", [D, 2 * DFF], F32, kind="ExternalInput").ap()
    w_f2 = dt("w_f2", [DFF, D], F32, kind="ExternalInput").ap()
    biasd = dt("biastab", [16, 128, 640], F32, kind="ExternalInput").ap()
    attng = dt("attng", [128, 1024], F32, kind="ExternalInput").ap()
    gng = dt("gng", [128, 128], F32, kind="ExternalInput").ap()
    ln1g = dt("ln1g", [128, D], F32, kind="ExternalInput").ap()
    ln1b = dt("ln1b", [128, D], F32, kind="ExternalInput").ap()
    ln2g = dt("ln2g", [128, D], F32, kind="ExternalInput").ap()
    ln2b = dt("ln2b", [128, D], F32, kind="ExternalInput").ap()
    cmaskd = dt("cmask", [128, 128], F32, kind="ExternalInput").ap()
    rmaskd = dt("rmask", [128, 512], F32, kind="ExternalInput").ap()
    identd = dt("ident", [128, 128], F32, kind="ExternalInput").ap()
    outd = dt("out", [TOK, D], F32, kind="ExternalOutput").ap()
    zscr = dt("zscr", [TOK, D], F32, kind="Internal").ap()
    dbg = {}
    if debug:
        for name, shape in debug.items():
            dbg[name] = dt("dbg_" + name, shape, F32, kind="ExternalOutput").ap()

    P = Prog(nc)
    es = ExitStack()
    with es:
        def sb(name, shape, dtype, stack=es):
            return stack.enter_context(nc.sbuf_tensor("sb_" + name, shape, dtype))

        def psum(name, shape, dtype, stack=es):
            return stack.enter_context(nc.psum_tensor("pp_" + name, shape, dtype))

        sems = {e: es.enter_context(nc.semaphore("s_" + e)) for e in ENGS}
        _dsn = [0]

        def dsem():
            _dsn[0] += 1
            return _DmaSem(es.enter_context(nc.semaphore("d%d" % _dsn[0])))

        RS = 3
        ringbuf = sb("ringbuf", [128, RS * 8192], BF16)
        ring = [ringbuf[:, i * 8192:(i + 1) * 8192] for i in range(RS)]
        ringE = [ringbuf[:, i * 12288:(i + 1) * 12288] for i in range(2)]
        ring_sem = [dsem() for _ in range(RS)]
        ringE_sem = [dsem() for _ in range(2)]
        ringE_n = [0]
        ring_n = [0]
        consts_sem = dsem()
        ARENA_B = 136 * 1024
        ident_f = sb("ident_f", [128, 128], F32)
        ident_b = sb("ident_b", [128, 128], BF16)
        ones_f = sb("ones_f", [128, 128], F32)
        valid = sb("valid", [128, NT], F32)
        modfm = sb("modfm", [128, 96], F32)
        sc1p = sb("sc1p", [128, 16], F32)
        sc2p = sb("sc2p", [128, 16], F32)
        epst = sb("epst", [128, 1], F32)
        lbv = sb("lbv", [128, 8], F32)
        oml = sb("oml", [128, 8], F32)
        cmask = sb("cmask", [128, 128], F32)
        rmask = sb("rmask", [128, 512], F32)
        gng_t = sb("gng_t", [128, 128], F32)
        attn_t = sb("attn_t", [128, 1024], F32)
        st6 = [sb("st6_%d" % i, [128, 4, 6], F32) for i in range(8)]
        mvt = [sb("mv_%d" % i, [128, 2], F32) for i in range(8)]
        sdt = [sb("sd_%d" % i, [128, 1], F32) for i in range(8)]
        rst = [sb("rs_%d" % i, [128, 1], F32) for i in range(8)]
        nmt = [sb("nm_%d" % i, [128, 1], F32) for i in range(8)]

        sm4 = sb("sm4", [128, 16], F32)
        onb = sb("onb", [128, 256], BF16)
        ones4 = sb("ones4", [128, 4], F32)
        mhalf = sb("mhalf", [128, 4], F32)
        Sbf = [sb("Sbf%d" % i, [128, 256], BF16) for i in range(6)]
        ebt = sb("ebt", [128, 2, 32], F32)
        junk = sb("junk", [128, 128], F32)
        tmpy = [sb("tmpy%d" % i, [128, 256], F32) for i in range(2)]
        sb_dg0 = sb("dg0", [128, 128], F32)
        sb_dg1 = sb("dg1", [128, 128], F32)
        ps = [psum("ps%d" % i, [128, 512], F32) for i in range(7)]
        psT = psum("psT", [128, 1024], BF16)

        _an = [0]

        def alloc(stack, shape, dtype, side=None):
            _an[0] += 1
            kw = {"side": side} if side else {}
            t = stack.enter_context(nc.sbuf_tensor("sb_a%d" % _an[0], shape, dtype, **kw))
            return t[:]

        def carve(off, shape, dtype):
            n = 1
            for d_ in shape[1:]:
                n *= d_
            nbytes = n * (2 if dtype == BF16 else 4)
            assert off % 4 == 0 and off + nbytes <= ARENA_B, (off, nbytes)
            v = arena[:, off // 4:(off + nbytes) // 4]
            if dtype == BF16:
                v = v.bitcast(BF16)
            if len(shape) == 3:
                v = v.rearrange("p (a b) -> p a b", a=shape[1])
            elif len(shape) == 4:
                v = v.rearrange("p (a b c) -> p a b c", a=shape[1], b=shape[2])
            return v

        KB = 1024

        def cload(dst, src, key):
            P.dma("sp", consts_sem, lambda e, d=dst, s=src: e.dma_start(out=d, in_=s), writes=[key])

        def OP(eng, fn, reads=(), writes=()):
            return P.op(eng, fn, reads, writes)

        def MM(out, lhsT, rhs, start, stop, reads, writes):
            return P.op("pe", lambda e: e.matmul(out, lhsT, rhs, start=start, stop=stop), reads, writes)

        def TR(out, in_, idt, reads, writes):
            return P.op("pe", lambda e: e.transpose(out, in_, idt), reads, writes)

        def ACT(out, in_, func, reads, writes, scale=None, bias=None, accum=None):
            kw = {}
            if scale is not None:
                kw["scale"] = scale
            if bias is not None:
                kw["bias"] = bias
            if accum is not None:
                kw["accum_out"] = accum
            return P.op("act", lambda e: e.activation(out=out, in_=in_, func=func, **kw), reads, writes)

        def TT(out, in0, in1, op, reads, writes, eng="dve"):
            return P.op(eng, lambda e: e.tensor_tensor(out=out, in0=in0, in1=in1, op=op), reads, writes)

        def TS(out, in0, s1, s2, op0, op1, reads, writes, eng="dve"):
            if op1 is None:
                return P.op(eng, lambda e: e.tensor_scalar(out=out, in0=in0, scalar1=s1, scalar2=None, op0=op0), reads, writes)
            return P.op(eng, lambda e: e.tensor_scalar(out=out, in0=in0, scalar1=s1, scalar2=s2, op0=op0, op1=op1), reads, writes)

        def STT(out, in0, scalar, in1, op0, op1, reads, writes):
            return P.op("dve", lambda e: e.scalar_tensor_tensor(out=out, in0=in0, scalar=scalar, in1=in1, op0=op0, op1=op1), reads, writes)

        def COPY(eng, out, in_, reads, writes):
            if eng == "act":
                return P.op("act", lambda e: e.activation(out=out, in_=in_, func=AF.Copy), reads, writes)
            return P.op(eng, lambda e: e.tensor_copy(out=out, in_=in_), reads, writes)

        def DMA(eng, sem, out, in_, reads, writes):
            return P.dma(eng, sem, lambda e: e.dma_start(out=out, in_=in_), reads, writes)

        cload(ident_f[:], identd, "ident_f")
        cload(valid[:], validd, "valid")
        cload(cmask[:], cmaskd, "cmask")
        cload(rmask[:], rmaskd, "rmask")
        cload(gng_t[:], gng, "gng")
        cload(attn_t[:], attng, "attn_t")
        OP("pool", lambda e: e.memset(epst[:], EPS), writes=["epst"])
        OP("pool", lambda e: e.memset(ones_f[:], 1.0), writes=["ones_f"])
        OP("pool", lambda e: e.memset(mhalf[:], -0.5), writes=["mhalf"])

        def ring_load(pieces, nk=16):
            s = ring_n[0] % RS
            ring_n[0] += 1
            wtot = sum(p[1] for p in pieces)
            view = ring[s][:, 0:nk * wtot].rearrange("p (k n) -> p k n", k=nk)
            key = "ring%d" % s
            step = 4
            first = True
            for (d0, wd, W, s0) in pieces:
                Wv = W.rearrange("(k p) n -> p k n", p=128)
                for k0 in range(0, nk, step):
                    k1 = min(nk, k0 + step)
                    DMA("pool", ring_sem[s], view[:, k0:k1, d0:d0 + wd], Wv[:, k0:k1, s0:s0 + wd],
                        [], [key] if first else [])
                    first = False
            P.reg[key] = [P.ops["pool"][-1], []]
            return view, key

        def ringE_load(W, col0, width, nk):
            s = ringE_n[0] % 2
            ringE_n[0] += 1
            view = ringE[s][:, 0:nk * width].rearrange("p (k n) -> p k n", k=nk)
            key = "ringE%d" % s
            Wv = W.rearrange("(k p) n -> p k n", p=128)
            first = True
            for k0 in range(0, nk, 4):
                k1 = min(nk, k0 + 4)
                DMA("pool", ringE_sem[s], view[:, k0:k1, :], Wv[:, k0:k1, col0:col0 + width], [], [key] if first else [])
                first = False
            P.reg[key] = [P.ops["pool"][-1], []]
            return view, key

        def ln_stats(src, srckey, par):
            k = "ln%d" % par
            for q in range(4):
                OP("dve", lambda e, q=q: e.bn_stats(out=st6[par][:, q, :], in_=src[:, q * 512:(q + 1) * 512]),
                   [srckey], [k + "st%d" % q])
            OP("dve", lambda e: e.bn_aggr(out=mvt[par][:], in_=st6[par][:].rearrange("p a b -> p (a b)")),
               [k + "st%d" % q for q in range(4)], [k + "mv"])
            ACT(sdt[par][:], mvt[par][:, 1:2], AF.Sqrt, [k + "mv", "epst"], [k + "sd"], scale=1.0, bias=epst[:, 0:1])
            OP("dve", lambda e: e.reciprocal(out=rst[par][:], in_=sdt[par][:]), [k + "sd"], [k + "rs"])
            TS(nmt[par][:], mvt[par][:, 0:1], rst[par][:, 0:1], -1.0, ALU.mult, ALU.mult, [k + "mv", k + "rs"], [k + "nm"])
            return [k + "rs", k + "nm"]

        def ln_a(src, srckey, par):
            k = "ln%d" % par
            for q in range(4):
                OP("dve", lambda e, q=q: e.bn_stats(out=st6[par][:, q, :], in_=src[:, q * 512:(q + 1) * 512]),
                   [srckey], [k + "st%d" % q])
            OP("dve", lambda e: e.bn_aggr(out=mvt[par][:], in_=st6[par][:].rearrange("p a b -> p (a b)")),
               [k + "st%d" % q for q in range(4)], [k + "mv"])

        def ln_b(par):
            k = "ln%d" % par
            ACT(sdt[par][:], mvt[par][:, 1:2], AF.Sqrt, [k + "mv", "epst"], [k + "sd"], scale=1.0, bias=epst[:, 0:1])

        def ln_c(par):
            k = "ln%d" % par
            OP("dve", lambda e: e.reciprocal(out=rst[par][:], in_=sdt[par][:]), [k + "sd"], [k + "rs"])
            TS(nmt[par][:], mvt[par][:, 0:1], rst[par][:, 0:1], -1.0, ALU.mult, ALU.mult, [k + "mv", k + "rs"], [k + "nm"])
            return [k + "rs", k + "nm"]

        def bc_from_fm(dst, dstkey, col0):
            dg = [sb_dg0, sb_dg1]
            for c in range(16):
                b_ = c % 2
                TS(dg[b_][:], ident_f[:], modfm[:, col0 + c:col0 + c + 1], None, ALU.mult, None,
                   ["ident_f", "modfm"], ["dg%d" % b_])
                MM(ps[b_][:, 0:128], ones_f[:], dg[b_][:], True, True, ["ones_f", "dg%d" % b_], ["psb%d" % b_])
                COPY("act", dst[:, c * 128:(c + 1) * 128], ps[b_][:, 0:128], ["psb%d" % b_], [dstkey + ".%d" % c])

        c_f = sb("c_f", [128, 16], F32)
        c_act = sb("c_act", [128, 16], BF16)
        bada = sb("bada", [128, 96], F32)
        lbl = sb("lbl", [128, 16], F32)
        cload(c_f[:], cfm, "c_f")
        cload(bada[:], badafm, "bada")
        cload(lbl[:], lbfm, "lbl")
        P.barrier()
        OP("dve", lambda e: e.tensor_copy(out=ident_b[:], in_=ident_f[:]), ["ident_f"], ["ident_b"])
        ACT(c_act[:], c_f[:], AF.Silu, ["c_f"], ["c_act"])
        TT(lbl[:, 0:8], lbl[:, 0:8], lbl[:, 8:16], ALU.subtract, ["lbl"], ["lbl"])
        ACT(lbv[:], lbl[:, 0:8], AF.Sigmoid, ["lbl"], ["lbv"])
        TS(oml[:], lbv[:], -1.0, 1.0, ALU.mult, ALU.add, ["lbv"], ["oml"])
        psm = ps[6]

        def ada_block(nb, dst=None, dkey="psm", c0=0):
            view, key = ring_load([(0, 512, w_ada, nb * 512)])
            dst = psm if dst is None else dst
            for j in range(4):
                col = nb * 4 + j - c0
                for k in range(16):
                    MM(dst[:, col:col + 1], view[:, k, j * 128:(j + 1) * 128], c_act[:, k:k + 1], k == 0, k == 15,
                       [key, "c_act"], [dkey])
        for nb in range(8):
            ada_block(nb)
        TT(modfm[:, 0:32], psm[:, 0:32], bada[:, 0:32], ALU.add, ["psm", "bada"], ["modfm"])
        TS(sc1p[:], modfm[:, 16:32], 1.0, None, ALU.add, None, ["modfm"], ["sc1p"])

        if stop_after == "0":
            P.disabled = True
        es_hT = ExitStack()
        es_oT = ExitStack()
        hT = alloc(es_hT, [128, 16, 2048], BF16)
        oT = alloc(es_oT, [128, 16, TOK], BF16, side="right")
        GB = 96 * KB

        def ln_mod_transpose(src, srckey, par, dstT, dstkey, tok0, scp, shcol, do_ln=True, act_only=False):
            if do_ln:
                keys = ln_stats(src, srckey, par)
                ACT(src, src, AF.Identity, [srckey] + keys, [srckey], scale=rst[par][:, 0:1], bias=nmt[par][:, 0:1])
            for g4 in range(4):
                bank = ps[g4]
                bkey = "psA%d" % g4
                for c4 in range(4):
                    c = g4 * 4 + c4
                    TR(bank[:, c4 * 128:(c4 + 1) * 128], src[:, c * 128:(c + 1) * 128], ident_f[:],
                       [srckey, "ident_f"], [bkey])
                for c4 in range(4):
                    c = g4 * 4 + c4
                    o_ = dstT[:, c, tok0:tok0 + 128]
                    i_ = bank[:, c4 * 128:(c4 + 1) * 128]
                    if g4 % 2 == 0 or act_only:
                        ACT(o_, i_, AF.Identity, [bkey, "sc1p", "modfm"], [dstkey + ".%d" % c],
                            scale=scp[:, c:c + 1], bias=modfm[:, shcol + c:shcol + c + 1])
                    else:
                        TS(o_, i_, scp[:, c:c + 1], modfm[:, shcol + c:shcol + c + 1], ALU.mult, ALU.add,
                           [bkey, "sc1p", "modfm"], [dstkey + ".%d" % c])

        es_a = ExitStack()
        xb = [alloc(es_a, [128, 2048], F32) for i in range(4)]
        xsem = [dsem() for _ in range(4)]

        def a_s1(TT_):
            for T in TT_:
                b_ = T % 4
                DMA("sp", xsem[b_], xb[b_], xs[T * 128:(T + 1) * 128, :], [], ["xb%d" % b_])
            for T in TT_:
                ln_a(xb[T % 4], "xb%d" % (T % 4), T % 8)
            for T in TT_:
                ln_b(T % 8)
            for T in TT_:
                ln_c(T % 8)
            for T in TT_:
                b_, par = T % 4, T % 8
                k = "ln%d" % par
                ACT(xb[b_], xb[b_], AF.Identity, ["xb%d" % b_, k + "rs", k + "nm"], ["xb%d" % b_],
                    scale=rst[par][:, 0:1], bias=nmt[par][:, 0:1])

        def a_s2(TT_):
            for T in TT_:
                ln_mod_transpose(xb[T % 4], "xb%d" % (T % 4), 0, hT, "hT.%d" % T, T * 128, sc1p, 0, do_ln=False)

        for p in range(8):
            a_s1([2 * p, 2 * p + 1])
            if p >= 1:
                a_s2([2 * p - 2, 2 * p - 1])
        a_s2([14, 15])
        P.barrier()
        es_a.close()

        def hkeys(T0, T1, kc):
            return ["hT.%d.%d" % (T, kc) for T in range(T0, T1)]

        if stop_after == "A":
            P.disabled = True
        es_b1 = ExitStack()
        kT = alloc(es_b1, [128, 2, 1536], BF16)
        vaug = alloc(es_b1, [128, 12, 4, 65], BF16)
        bhi = alloc(es_b1, [128, 4, 640], BF16)
        blo = alloc(es_b1, [128, 4, 640], BF16)
        bstg2 = [alloc(es_b1, [128, 640], F32) for i in range(2)]
        bsem2 = [dsem(), dsem()]
        qTz = alloc(es_b1, [128, 4, 512], BF16)
        pT = [alloc(es_b1, [128, 4, 640], BF16) for i in range(2)]
        onba = [alloc(es_b1, [128, 256], BF16) for i in range(2)]
        o_sb = alloc(es_b1, [128, 4, 64], F32)
        o_sq = alloc(es_b1, [128, 4, 64], F32)
        bsem = dsem()
        OP("pool", lambda e: e.memset(ones4[:], 1.0), [], ["ones4"])
        for t in range(12):
            ACT(vaug[:, t, :, 64:65], ones4[:].unsqueeze(2), AF.Identity, ["ones4", "valid"], ["vaug1.%d" % t],
                scale=valid[:, t + 4:t + 5])
        OP("dve", lambda e: e.memset(qTz, 0.0), [], ["qTz.%d" % h for h in range(4)])
        for ga in range(4):
            kv, kvkey = ring_load([(0, 256, w_in, 1024 + ga * 256), (256, 256, w_in, 2048 + ga * 256)])
            qv, qkey = ring_load([(0, 256, w_in, ga * 256)])
            for h in range(4):
                bs, bk = bstg2[h % 2], "bstg%d" % (h % 2)
                DMA("sp", bsem2[h % 2], bs, biasd[ga * 4 + h], [], [bk])
                COPY("dve", bhi[:, h, :], bs, [bk], ["bhi.%d" % h])
                TT(blo[:, h, :], bs, bhi[:, h, :], ALU.subtract, [bk, "bhi.%d" % h], ["blo.%d" % h])
            nproj = [0]

            def proj_bank():
                i = nproj[0] % 2
                nproj[0] += 1
                return ps[i], "psP%d" % i
            for tg in range(1, 4):
                for pr in range(2):
                    bank, bkey = proj_bank()
                    for kc in range(16):
                        MM(bank[:, 0:512], kv[:, kc, pr * 128:(pr + 1) * 128], hT[:, kc, tg * 512:(tg + 1) * 512],
                           kc == 0, kc == 15, [kvkey] + hkeys(tg * 4, tg * 4 + 4, kc), [bkey])
                    COPY("act" if pr == 0 else "dve", kT[:, pr, (tg - 1) * 512:tg * 512], bank[:, 0:512],
                         [bkey], ["kT.%d.%d" % (pr, tg)])
            for T in range(4, 16):
                bank, bkey = proj_bank()
                for kc in range(16):
                    MM(bank[:, 0:256], hT[:, kc, T * 128:(T + 1) * 128], kv[:, kc, 256:512], kc == 0, kc == 15,
                       [kvkey, "hT.%d.%d" % (T, kc)], [bkey])
                ACT(vaug[:, T - 4, :, 0:64], bank[:, 0:256].rearrange("p (h d) -> p h d", h=4), AF.Identity,
                    [bkey, "valid"], ["vaug.%d" % (T - 4)], scale=valid[:, T:T + 1])
            def att_scores(T, ti):
                pb = T % 2
                for h in range(4):
                    pr, hf = h // 2, h % 2
                    sbank = ps[2 + h % 2]
                    skey = "psS%d" % (h % 2)
                    bbank = ps[4 if hf == 0 else 6]
                    bkey_ = "psB%d" % hf
                    MM(sbank[:, 0:512], ident_b[:], bhi[:, h, 0:512], True, False, ["ident_b", "bhi.%d" % h], [skey])
                    MM(sbank[:, 0:512], ident_b[:], blo[:, h, 0:512], False, False, ["ident_b", "blo.%d" % h], [skey])
                    for jb in range(4):
                        off = (T - 8 + jb) * 128
                        tgk = 1 + off // 512
                        MM(sbank[:, jb * 128:(jb + 1) * 128], kT[:, pr, off:off + 128], qTz[:, h, ti * 128:(ti + 1) * 128],
                           False, jb == 3, ["kT.%d.%d" % (pr, tgk), "qTz.%d" % h], [skey])
                    off = (T - 8 + 4) * 128
                    tgk = 1 + off // 512
                    bo = bbank[:, pr * 128:(pr + 1) * 128]
                    MM(bo, ident_b[:], bhi[:, h, 512:640], True, False, ["ident_b", "bhi.%d" % h], [bkey_])
                    MM(bo, ident_b[:], blo[:, h, 512:640], False, False, ["ident_b", "blo.%d" % h], [bkey_])
                    MM(bo, kT[:, pr, off:off + 128], qTz[:, h, ti * 128:(ti + 1) * 128], False, True,
                       ["kT.%d.%d" % (pr, tgk), "qTz.%d" % h], [bkey_])
                    ACT(pT[pb][:, h, 0:512], sbank[:, 0:512], AF.Exp, [skey], ["pT%d.%d" % (pb, h)])
                for hf in range(2):
                    ACT(pT[pb][:, hf:4:2, 512:640], ps[4 if hf == 0 else 6][:, 0:256].rearrange("p (h t) -> p h t", h=2),
                        AF.Exp, ["psB%d" % hf], ["pTb%d.%d" % (pb, hf), "pTb%d.%d" % (pb, hf + 2)])

            def att_pv(T):
                pb = T % 2
                for h in range(4):
                    for jb in range(5):
                        MM(ps[5][:, h * 65:(h + 1) * 65], pT[pb][:, h, jb * 128:(jb + 1) * 128], vaug[:, T - 8 + jb, h, :],
                           jb == 0, jb == 4,
                           ["pT%d.%d" % (pb, h), "pTb%d.%d" % (pb, h), "vaug.%d" % (T - 8 + jb), "vaug1.%d" % (T - 8 + jb)],
                           ["psO"])
                pso = ps[5][:, 0:260].rearrange("p (h d) -> p h d", h=4)
                OP("dve", lambda e, pso=pso: e.reciprocal(out=sm4[:, 0:4], in_=pso[:, :, 64]), ["psO"], ["sm4a"])
                TT(o_sb, pso[:, :, 0:64], sm4[:, 0:4].unsqueeze(2).to_broadcast([128, 4, 64]), ALU.mult,
                   ["psO", "sm4a"], ["o_sb"])
                TT(o_sq, o_sb, o_sb, ALU.mult, ["o_sb"], ["o_sq"])
                OP("dve", lambda e: e.tensor_reduce(out=sm4[:, 4:8], in_=o_sq, axis=AX.X, op=ALU.add), ["o_sq"], ["sm4b"])
                ACT(sm4[:, 8:12], sm4[:, 4:8], AF.Ln, ["sm4b", "epst"], ["sm4c"], scale=1.0 / 64, bias=epst[:, 0:1])
                ACT(sm4[:, 12:16], sm4[:, 8:12], AF.Exp, ["sm4c"], ["sm4d"], scale=-0.5)
                TT(o_sb, o_sb, sm4[:, 12:16].unsqueeze(2).to_broadcast([128, 4, 64]), ALU.mult, ["o_sb", "sm4d"], ["o_sb"])
                TT(onba[pb].rearrange("p (h d) -> p h d", h=4), o_sb,
                   attn_t[:, ga * 256:(ga + 1) * 256].rearrange("p (h d) -> p h d", h=4), ALU.mult,
                   ["o_sb", "attn_t"], ["onba%d" % pb])

            def att_tr(T):
                pb = T % 2
                for pr in range(2):
                    TR(psT[:, pr * 128:(pr + 1) * 128], onba[pb][:, pr * 128:(pr + 1) * 128], ident_b[:],
                       ["onba%d" % pb, "ident_b"], ["psT"])
                COPY("act", oT[:, ga * 2:ga * 2 + 2, (T - 8) * 128:(T - 7) * 128],
                     psT[:, 0:256].rearrange("p (a t) -> p a t", a=2), ["psT"], ["oT.%d.%d" % (ga, T - 8)])

            for tq in range(2):
                for pr in range(2):
                    bank, bkey = proj_bank()
                    for kc in range(16):
                        MM(bank[:, 0:512], qv[:, kc, pr * 128:(pr + 1) * 128],
                           hT[:, kc, 1024 + tq * 512:1024 + (tq + 1) * 512], kc == 0, kc == 15,
                           [qkey] + hkeys(8 + tq * 4, 12 + tq * 4, kc), [bkey])
                    for hf in range(2):
                        ACT(qTz[hf * 64:(hf + 1) * 64, 2 * pr + hf, :], bank[hf * 64:(hf + 1) * 64, 0:512], AF.Copy, [bkey],
                            ["qTz.%d" % (2 * pr + hf)], scale=0.125)
                for ti in range(4):
                    T = 8 + tq * 4 + ti
                    att_scores(T, ti)
                    if (T - 8) in (1, 4, 6):
                        ada_block(8 + ga * 3 + {1: 0, 4: 1, 6: 2}[T - 8], dst=ps[5][:, 384:448], dkey="psO", c0=32)
                    if T - 1 >= 8:
                        att_pv(T - 1)
                    if T - 2 >= 8:
                        att_tr(T - 2)
            att_pv(15)
            att_tr(14)
            att_tr(15)
        TT(modfm[:, 32:80], ps[5][:, 384:432], bada[:, 32:80], ALU.add, ["psO", "bada"], ["modfmB"])
        TS(sc2p[:], modfm[:, 64:80], 1.0, None, ALU.add, None, ["modfmB"], ["sc2p"])
        P.barrier()

        if stop_after == "B1":
            P.disabled = True
        es_b1.close()
        es_b2 = ExitStack()
        nkTm = alloc(es_b2, [128, 2, 1024], BF16)
        qtT = alloc(es_b2, [128, 2, 1024], BF16)
        nktok = alloc(es_b2, [128, 16, 256], BF16)
        w1 = [alloc(es_b2, [128, 512], F32) for i in range(2)]
        w2 = [alloc(es_b2, [128, 512], F32) for i in range(2)]
        w3 = [alloc(es_b2, [128, 512], F32) for i in range(2)]
        nkh = [alloc(es_b2, [128, 512], BF16) for i in range(2)]
        isb = [alloc(es_b2, [128, 256], BF16) for i in range(3)]
        sgt = [alloc(es_b2, [128, 256], F32) for i in range(3)]
        atsb = [alloc(es_b2, [128, 256], BF16) for i in range(2)]
        onb2 = [alloc(es_b2, [128, 256], BF16) for i in range(2)]
        Sst2 = [alloc(es_b2, [128, 256], F32) for i in range(2)]
        tmpS2 = [alloc(es_b2, [128, 256], F32) for i in range(2)]
        onf = alloc(es_b2, [128, 256], F32)
        def rec_loads(g):
            a = ring_load([(0, 256, w_in, 4096 + g * 256), (256, 256, w_in, 3072 + g * 256)])
            b = ring_load([(0, 256, w_in, 6144 + g * 256), (256, 256, w_in, 5120 + g * 256)])
            return a, b
        nxt_loads = rec_loads(0)
        for gr in range(4):
            (fq, fqkey), (gi, gikey) = nxt_loads
            npj = [0]

            def fbank():
                i = npj[0] % 2
                npj[0] += 1
                return ps[i], "psF%d" % i
            for tg in range(4):
                main = tg >= 2
                HH = range(2)
                tsl = slice(tg * 512, (tg + 1) * 512)
                nkd = [nkTm[:, hh, (tg - 2) * 512:(tg - 1) * 512] if main else nkh[hh] for hh in HH]
                nkk = ["nkTm.%d.%d" % (hh, tg) if main else "nkh%d" % hh for hh in HH]
                for hh in HH:
                    for kc in range(16):
                        MM(ps[hh][:, 0:512], fq[:, kc, hh * 128:(hh + 1) * 128], hT[:, kc, tsl],
                           kc == 0, kc == 15, [fqkey] + hkeys(tg * 4, tg * 4 + 4, kc), ["psF%d" % hh])
                for hh in HH:
                    ACT(w1[hh], ps[hh][:, 0:512], AF.Sigmoid, ["psF%d" % hh], ["w1.%d" % hh])
                for hh in HH:
                    hd = gr * 2 + hh
                    TS(w1[hh], w1[hh], oml[:, hd:hd + 1], lbv[:, hd:hd + 1], ALU.mult, ALU.add,
                       ["w1.%d" % hh, "oml", "lbv"], ["w1.%d" % hh])
                for hh in HH:
                    ACT(w2[hh], w1[hh], AF.Ln, ["w1.%d" % hh], ["w2.%d" % hh])
                for hh in HH:
                    OP("dve", lambda e, hh=hh: e.tensor_tensor_scan(out=w3[hh], data0=rmask[:], data1=w2[hh], initial=0.0,
                                                                   op0=ALU.mult, op1=ALU.add),
                       ["rmask", "w2.%d" % hh], ["w3.%d" % hh])
                for hh in HH:
                    ACT(w2[hh], w3[hh], AF.Exp, ["w3.%d" % hh], ["w2.%d" % hh], scale=-1.0)
                for hh in HH:
                    STT(nkd[hh], w1[hh], 1.0, w2[hh], ALU.subtract, ALU.mult, ["w1.%d" % hh, "w2.%d" % hh], [nkk[hh]])
                for hh in HH:
                    ACT(ebt[:, hh, tg * 8:(tg + 1) * 8], w3[hh][:, 63:512:64], AF.Exp, ["w3.%d" % hh],
                        ["ebt.%d.%d" % (hh, tg)])
                for hh in HH:
                    for j in range(4):
                        TR(psT[:, hh * 512 + j * 128:hh * 512 + (j + 1) * 128], nkd[hh][:, j * 128:(j + 1) * 128], ident_b[:],
                           [nkk[hh], "ident_b"], ["psT"])
                for hh in HH:
                    COPY("dve", nktok[:, tg * 4:(tg + 1) * 4, hh * 128:(hh + 1) * 128],
                         psT[:, hh * 512:(hh + 1) * 512].rearrange("p (j d) -> p j d", j=4), ["psT"],
                         ["nktok.%d.%d" % (tg, hh)])
                if main:
                    for hh in HH:
                        for kc in range(16):
                            MM(ps[hh][:, 0:512], fq[:, kc, 256 + hh * 128:256 + (hh + 1) * 128], hT[:, kc, tsl],
                               kc == 0, kc == 15, [fqkey] + hkeys(tg * 4, tg * 4 + 4, kc), ["psF%d" % hh])
                    for hh in HH:
                        ACT(w1[hh], ps[hh][:, 0:512], AF.Silu, ["psF%d" % hh], ["w1.%d" % hh])
                    for hh in HH:
                        ACT(w2[hh], w3[hh], AF.Exp, ["w3.%d" % hh], ["w2.%d" % hh])
                    for hh in HH:
                        TT(qtT[:, hh, (tg - 2) * 512:(tg - 1) * 512], w1[hh], w2[hh], ALU.mult,
                           ["w1.%d" % hh, "w2.%d" % hh], ["qtT.%d.%d" % (hh, tg)])
            if gr + 1 < 4:
                nxt_loads = rec_loads(gr + 1)
            OP("dve", lambda e: e.memset(Sst2[0], 0.0), [], ["S0.0", "S0.1"])
            OP("dve", lambda e: e.memset(Sbf[0][:], 0.0), [], ["Sbf0.0", "Sbf0.1"])

            def part_at(T):
                tgm = T // 4
                c0 = (T % 2) * 256
                for hh in range(2):
                    MM(ps[5][:, c0 + hh * 128:c0 + (hh + 1) * 128], nkTm[:, hh, (T - 8) * 128:(T - 7) * 128],
                       qtT[:, hh, (T - 8) * 128:(T - 7) * 128], True, True,
                       ["nkTm.%d.%d" % (hh, tgm), "qtT.%d.%d" % (hh, tgm)], ["psAT"])

            def part_mask(T):
                c0 = (T % 2) * 256
                TT(atsb[T % 2].rearrange("p (h t) -> p h t", h=2), ps[5][:, c0:c0 + 256].rearrange("p (h t) -> p h t", h=2),
                   cmask[:].unsqueeze(1).to_broadcast([128, 2, 128]), ALU.mult, ["psAT", "cmask"], ["atsb%d" % (T % 2)])

            def part_o(T):
                ib = isb[T % 3]
                ibk = "isb%d" % (T % 3)
                tgm = T // 4
                at = atsb[T % 2]
                ob_ = onb2[T % 2]
                p_ = T % 2
                pob = ps[6] if p_ == 0 else ps[0]
                pkey = "psOB" if p_ == 0 else "psF0"
                a0, t0_, r0 = 2 * p_, 4 + 2 * p_, 8 + 2 * p_
                for hh in range(2):
                    MM(pob[:, hh * 128:(hh + 1) * 128], at[:, hh * 128:(hh + 1) * 128], ib[:, hh * 128:(hh + 1) * 128],
                       True, False, ["atsb%d" % (T % 2), ibk], [pkey])
                    for c2 in range(2):
                        ci = 2 * T + c2
                        t0 = (T - 8) * 128 + c2 * 64
                        MM(pob[c2 * 64:(c2 + 1) * 64, hh * 128:(hh + 1) * 128], qtT[:, hh, t0:t0 + 64],
                           Sbf[ci % 6][:, hh * 128:(hh + 1) * 128], False, True,
                           ["qtT.%d.%d" % (hh, tgm), "Sbf%d.%d" % (ci % 6, hh)], [pkey])
                for hh in range(2):
                    ACT(junk[:], pob[:, hh * 128:(hh + 1) * 128], AF.Square, [pkey], ["junk", "sm4r%d.%d" % (p_, hh)],
                        accum=sm4[:, a0 + hh:a0 + hh + 1])
                TS(sm4[:, t0_:t0_ + 2], sm4[:, a0:a0 + 2], 1.0 / 128, EPS, ALU.mult, ALU.add,
                   ["sm4r%d.0" % p_, "sm4r%d.1" % p_], ["sm4s%d" % p_], eng="pool")
                TT(sm4[:, r0:r0 + 2], sm4[:, t0_:t0_ + 2], mhalf[:, 0:2], ALU.pow, ["sm4s%d" % p_, "mhalf"], ["sm4t%d" % p_],
                   eng="pool")
                for hh in range(2):
                    STT(onf[:, hh * 128:(hh + 1) * 128], pob[:, hh * 128:(hh + 1) * 128], sm4[:, r0 + hh:r0 + hh + 1], gng_t[:],
                        ALU.mult, ALU.mult, [pkey, "sm4t%d" % p_, "gng"], ["onf.%d" % hh])
                TT(ob_, onf, sgt[T % 3], ALU.mult, ["onf.0", "onf.1", "sgt%d" % (T % 3)], ["onb2_%d" % (T % 2)])

            def part_tr(T):
                ob_ = onb2[T % 2]
                for hh in range(2):
                    TR(psT[:, hh * 128:(hh + 1) * 128], ob_[:, hh * 128:(hh + 1) * 128], ident_b[:],
                       ["onb2_%d" % (T % 2), "ident_b"], ["psT"])
                COPY("act", oT[:, 8 + gr * 2:10 + gr * 2, (T - 8) * 128:(T - 7) * 128],
                     psT[:, 0:256].rearrange("p (a t) -> p a t", a=2), ["psT"], ["oT.%d.%d" % (8 + gr, T - 8)])

            def gi_proj(T):
                main = T >= 8
                bank = ps[2 + T % 2]
                bkey = "psGI%d" % (T % 2)
                n = 512 if main else 256
                rhs_lo = 0 if main else 256
                for kc in range(16):
                    MM(bank[:, 0:n], hT[:, kc, T * 128:(T + 1) * 128], gi[:, kc, rhs_lo:512], kc == 0, kc == 15,
                       [gikey, "hT.%d.%d" % (T, kc)], [bkey])
                if main:
                    part_at(T)
                ib = isb[T % 3]
                ibk = "isb%d" % (T % 3)
                ACT(ib, bank[:, n - 256:n], AF.Identity, [bkey, "valid"], [ibk], scale=valid[:, T:T + 1])
                if main:
                    ACT(sgt[T % 3], bank[:, 0:256], AF.Silu, [bkey], ["sgt%d" % (T % 3)])

            def gi_scan(T):
                ib = isb[T % 3]
                ibk = "isb%d" % (T % 3)
                for c2 in range(2):
                    ci = 2 * T + c2
                    dsb = ps[4] if c2 == 0 else ps[1]
                    dsl = dsb[:, 0:256]
                    for hh in range(2):
                        MM(dsb[:, hh * 128:(hh + 1) * 128],
                           nktok[c2 * 64:(c2 + 1) * 64, T, hh * 128:(hh + 1) * 128],
                           ib[c2 * 64:(c2 + 1) * 64, hh * 128:(hh + 1) * 128], True, True,
                           ["nktok.%d.%d" % (T // 4, hh), ibk], ["psDS0" if c2 == 0 else "psF1"])
                    tb = tmpS2[ci % 2]
                    tbk = "tmpS%d" % (ci % 2)
                    TT(tb.rearrange("p (h e) -> p h e", h=2), dsl.rearrange("p (h e) -> p h e", h=2),
                       ebt[:, :, ci:ci + 1].to_broadcast([128, 2, 128]), ALU.mult,
                       ["psDS0" if c2 == 0 else "psF1", "ebt.0.%d" % (T // 4), "ebt.1.%d" % (T // 4)], [tbk])
                    nxt = (ci + 1) % 6
                    for hh in range(2):
                        ebap = ebt[:, hh, ci:ci + 1]
                        Sin, Sout = Sst2[ci % 2], Sst2[(ci + 1) % 2]
                        STT(Sout[:, hh * 128:(hh + 1) * 128], Sin[:, hh * 128:(hh + 1) * 128], ebap,
                            tb[:, hh * 128:(hh + 1) * 128], ALU.mult, ALU.subtract,
                            ["S%d.%d" % (ci % 2, hh), tbk, "ebt.%d.%d" % (hh, T // 4)], ["S%d.%d" % ((ci + 1) % 2, hh)])
                        TS(Sbf[nxt][:, hh * 128:(hh + 1) * 128], Sout[:, hh * 128:(hh + 1) * 128], 1.0, 1.0,
                           ALU.mult, ALU.mult, ["S%d.%d" % ((ci + 1) % 2, hh)], ["Sbf%d.%d" % (nxt, hh)], eng="pool")

            gi_proj(0)
            for T in range(NT):
                if T + 1 < NT:
                    gi_proj(T + 1)
                gi_scan(T)
                if T >= 8:
                    part_mask(T)
                if T - 1 >= 8:
                    part_o(T - 1)
                if T - 2 >= 8:
                    part_tr(T - 2)
            part_o(NT - 1)
            part_tr(NT - 2)
            part_tr(NT - 1)
        P.barrier()
        if stop_after == "B2":
            P.disabled = True
        if debug:
            dsd = dsem()
            if "hT8" in dbg:
                DMA("pool", dsd, dbg["hT8"], hT[:, :, 1024:1152], [], [])
            if "oT" in dbg:
                DMA("pool", dsd, dbg["oT"], oT, [], [])
            P.barrier()
        es_b2.close()
        es_hT.close()
        es_x1 = ExitStack()
        es_c1 = ExitStack()
        x1p = alloc(es_x1, [128, 8, 2048], F32)
        g1bc = alloc(es_c1, [128, 2048], F32)
        xp = [alloc(es_c1, [128, 512], F32) for i in range(2)]
        xpsem = [dsem(), dsem()]
        bc_from_fm(g1bc, "g1bc", 32)
        n_ = 0
        for nb in range(4):
            wv, wkey = ring_load([(0, 512, w_o, nb * 512)])
            for T in range(8):
                b_ = n_ % 2
                n_ += 1
                DMA("sp", xpsem[b_], xp[b_], xs[1024 + T * 128:1024 + (T + 1) * 128, nb * 512:(nb + 1) * 512],
                    [], ["xp%d" % b_])
                bank, bkey = ps[2 + b_], "psC%d" % b_
                for kc in range(16):
                    MM(bank[:, 0:512], oT[:, kc, T * 128:(T + 1) * 128], wv[:, kc, :], kc == 0, kc == 15, [wkey], [bkey])
                dst = x1p[:, T, nb * 512:(nb + 1) * 512]
                dk = "x1p.%d.%d" % (T, nb)
                TT(dst, bank[:, 0:512], g1bc[:, nb * 512:(nb + 1) * 512], ALU.mult,
                   [bkey] + ["g1bc.%d" % c for c in range(nb * 4, nb * 4 + 4)], [dk])
                STT(dst, xp[b_], ALPHA, dst, ALU.mult, ALU.add, ["xp%d" % b_, dk], [dk])
        P.barrier()

        if stop_after == "C1":
            P.disabled = True
        es_c1.close()
        es_oT.close()
        es_c2 = ExitStack()
        es_h2 = ExitStack()
        ln1g_t = alloc(es_c2, [128, 2048], F32)
        ln1b_t = alloc(es_c2, [128, 2048], F32)
        xn2 = [alloc(es_c2, [128, 2048], F32) for i in range(2)]
        h2T = alloc(es_h2, [128, 16, TOK], BF16, side="right")
        lsem = dsem()
        DMA("sp", lsem, ln1g_t, ln1g, [], ["lnp"])
        DMA("sp", lsem, ln1b_t, ln1b, [], [])
        P.reg["lnp"] = [P.ops["sp"][-1], []]
        stsem = [dsem(), dsem()]
        for nb in range(20, 24):
            ada_block(nb)

        def ln_affine(src, skey, par, g_t, b_t, add_eng="dve"):
            keys = ln_stats(src, skey, par)
            ACT(src, src, AF.Identity, [skey] + keys, [skey], scale=rst[par][:, 0:1], bias=nmt[par][:, 0:1])
            TT(src, src, g_t, ALU.mult, [skey, "lnp"], [skey])
            TT(src, src, b_t, ALU.add, [skey, "lnp"], [skey], eng=add_eng)

        X1 = [(x1p[:, T, :], "x1.%d" % T) for T in range(8)]
        for T in range(8):
            ln_a(X1[T][0], X1[T][1], T)
        for T in range(8):
            ln_b(T)
        for T in range(8):
            ln_c(T)
        for T in range(8):
            src, skey = X1[T]
            k = "ln%d" % T
            ACT(src, src, AF.Identity, [skey, k + "rs", k + "nm"], [skey], scale=rst[T][:, 0:1], bias=nmt[T][:, 0:1])
            TT(src, src, ln1g_t, ALU.mult, [skey, "lnp"], [skey])
            TT(src, src, ln1b_t, ALU.add, [skey, "lnp"], [skey])
            DMA("sp", stsem[T % 2], zscr[T * 128:(T + 1) * 128, :], src, [skey], ["zs.%d" % T])
        for T in range(8):
            ln_a(X1[T][0], X1[T][1], T)
        for T in range(8):
            ln_b(T)
        for T in range(8):
            ln_c(T)
        for T in range(8):
            src, skey = X1[T]
            par = T % 2
            k = "ln%d" % T
            ACT(xn2[par], src, AF.Identity, [skey, k + "rs", k + "nm"], ["xn2_%d" % par], scale=rst[T][:, 0:1],
                bias=nmt[T][:, 0:1])
            ln_mod_transpose(xn2[par], "xn2_%d" % par, par, h2T, "h2T.%d" % T, T * 128, sc2p, 48, do_ln=False)
        TT(modfm[:, 80:96], psm[:, 80:96], bada[:, 80:96], ALU.add, ["psm", "bada"], ["modfmD"])
        P.barrier()

        if stop_after == "C2":
            P.disabled = True
        es_c2.close()
        es_x1.close()
        es_act = ExitStack()
        es_d = ExitStack()
        actT = alloc(es_act, [128, 44, TOK], BF16)
        sgb = [alloc(es_d, [128, 512], F32) for i in range(2)]
        n_ = 0
        for j2 in range(22):
            wv, wkey = ring_load([(0, 128, w_f1, (2 * j2) * 128), (128, 128, w_f1, DFF + (2 * j2) * 128),
                                  (256, 128, w_f1, (2 * j2 + 1) * 128), (384, 128, w_f1, DFF + (2 * j2 + 1) * 128)])
            for jj in range(2):
                j = 2 * j2 + jj
                for th in range(2):
                    bi = n_ % 2
                    n_ += 1
                    bG, kG = ps[2 * bi], "psG%d" % bi
                    bU, kU = ps[2 * bi + 1], "psU%d" % bi
                    for kc in range(16):
                        MM(bG[:, 0:512], wv[:, kc, jj * 256:jj * 256 + 128], h2T[:, kc, th * 512:(th + 1) * 512],
                           kc == 0, kc == 15, [wkey], [kG])
                    for kc in range(16):
                        MM(bU[:, 0:512], wv[:, kc, jj * 256 + 128:jj * 256 + 256], h2T[:, kc, th * 512:(th + 1) * 512],
                           kc == 0, kc == 15, [wkey], [kU])
                    ACT(sgb[bi], bG[:, 0:512], AF.Silu, [kG], ["sgb%d" % bi])
                    TT(actT[:, j, th * 512:(th + 1) * 512], sgb[bi], bU[:, 0:512], ALU.mult, ["sgb%d" % bi, kU],
                       ["actT.%d.%d" % (j, th)])
        P.barrier(pool=True)

        if stop_after == "D":
            P.disabled = True
        es_d.close()
        es_h2.close()
        es_e = ExitStack()
        g2bc = alloc(es_e, [128, 2048], F32, side="right")
        ln2g_t = alloc(es_e, [128, 2048], F32, side="right")
        ln2b_t = alloc(es_e, [128, 2048], F32, side="right")
        es_e1 = ExitStack()
        zb = [alloc(es_e1, [128, 8, 256], F32, side="right") for i in range(2)]
        DMA("sp", lsem, ln2g_t, ln2g, [], ["lnp"])
        DMA("sp", lsem, ln2b_t, ln2b, [], [])
        P.reg["lnp"] = [P.ops["sp"][-1], []]
        bc_from_fm(g2bc, "g2bc", 80)
        zv = zscr.rearrange("(t p) n -> p t n", p=128)
        zlsem = [dsem(), dsem()]
        zssem = [dsem(), dsem()]
        for nbk in range(8):
            wv, wkey = ringE_load(w_f2, nbk * 256, 256, 44)
            b_ = nbk % 2
            zkeys = ["zb%d.%d" % (b_, T) for T in range(8)]
            DMA("sp", zlsem[b_], zb[b_], zv[:, :, nbk * 256:(nbk + 1) * 256], [], zkeys)
            for T in range(8):
                bank, bkey = ps[2 + T % 2], "psY%d" % (T % 2)
                for kc in range(44):
                    MM(bank[:, 0:256], actT[:, kc, T * 128:(T + 1) * 128], wv[:, kc, :], kc == 0, kc == 43, [wkey], [bkey])
                TT(tmpy[T % 2][:], bank[:, 0:256], g2bc[:, nbk * 256:(nbk + 1) * 256], ALU.mult,
                   [bkey, "g2bc.%d" % (2 * nbk), "g2bc.%d" % (2 * nbk + 1)], ["tmpy%d" % (T % 2)])
                STT(zb[b_][:, T, :], zb[b_][:, T, :], ALPHA, tmpy[T % 2][:], ALU.mult, ALU.add,
                    ["tmpy%d" % (T % 2), zkeys[T]], [zkeys[T]])
            DMA("sp", zssem[b_], zv[:, :, nbk * 256:(nbk + 1) * 256], zb[b_], zkeys, [])
        P.barrier()
        es_e1.close()
        es_act.close()
        es_e2 = ExitStack()
        ob = [alloc(es_e2, [128, 2048], F32) for i in range(8)]
        olsem = [dsem() for _ in range(8)]
        ossem = [dsem() for _ in range(8)]
        for T in range(8):
            DMA("sp", olsem[T], ob[T], zscr[T * 128:(T + 1) * 128, :], [], ["ob%d" % T])
        for T in range(8):
            ln_a(ob[T], "ob%d" % T, T)
        for T in range(8):
            ln_b(T)
        for T in range(8):
            ln_c(T)
        for T in range(8):
            okey = "ob%d" % T
            k = "ln%d" % T
            ACT(ob[T], ob[T], AF.Identity, [okey, k + "rs", k + "nm"], [okey], scale=rst[T][:, 0:1], bias=nmt[T][:, 0:1])
            TT(ob[T], ob[T], ln2g_t, ALU.mult, [okey, "lnp"], [okey])
            TT(ob[T], ob[T], ln2b_t, ALU.add, [okey, "lnp"], [okey], eng=("pool" if T == 7 else "dve"))
            DMA("sp", ossem[T], outd[T * 128:(T + 1) * 128, :], ob[T], [okey], [])
        if stop_after is not None:
            P.disabled = False
            fsem = dsem()
            DMA("sp", fsem, outd[0:128, 0:128], ident_f[:], [], [])
        P.barrier()
        OP("sp", lambda e: e.nop())
        OP("act", lambda e: e.nop())

        with nc.Block() as block:
            P.finalize(block, sems)
        es_e2.close()
        es_e.close()
    return nc


def _host_inputs(inputs):
    x = np.asarray(inputs["x"], np.float32)
    c = np.asarray(inputs["c"], np.float32)
    rel = np.asarray(inputs["rel_bias"], np.float32)[0]
    jj = np.arange(128)[:, None]
    u = np.arange(640)[None, :]
    jb = u // 128
    tt = u % 128
    relidx = 512 + tt - jb * 128 - jj
    idx = np.clip(relidx, -256, 256) + 256
    kc = (jb * 128 + jj) // 64
    qc = 8 + tt // 64
    ok = (qc - kc >= 0) & (qc - kc <= 8)
    rel_ext = np.concatenate([rel, np.full((16, 1), NEG, np.float32)], axis=1)
    idx = np.where(ok, idx, 513)
    biastab = np.ascontiguousarray(rel_ext[:, idx])
    s = np.arange(128)[:, None]
    t = np.arange(128)[None, :]
    cmask = np.where((s // 64 == t // 64) & (s <= t), -1.0, 0.0).astype(np.float32)
    rmask = np.tile((np.arange(512) % 64 != 0).astype(np.float32)[None, :], (128, 1))
    ident = np.eye(128, dtype=np.float32)
    bc = lambda v, n: np.ascontiguousarray(np.broadcast_to(np.asarray(v, np.float32).reshape(1, n), (128, n)))
    lb = np.asarray(inputs["lb_logits"], np.float32)
    lbfm = np.concatenate([lb[0].reshape(8, 128).T, lb[1].reshape(8, 128).T], axis=1)
    shared = {
        "badafm": np.ascontiguousarray(np.asarray(inputs["b_ada"], np.float32)[0].reshape(96, 128).T),
        "lbfm": np.ascontiguousarray(lbfm),
        "w_ada": np.ascontiguousarray(np.asarray(inputs["w_ada"], np.float32)[0]),
        "w_in": np.ascontiguousarray(np.asarray(inputs["w_in"], np.float32)[0]),
        "w_o": np.ascontiguousarray(np.asarray(inputs["w_o"], np.float32)[0]),
        "w_f1": np.ascontiguousarray(np.asarray(inputs["w_ffn_in"], np.float32)[0]),
        "w_f2": np.ascontiguousarray(np.asarray(inputs["w_ffn_out"], np.float32)[0]),
        "biastab": biastab,
        "attng": bc(np.asarray(inputs["attn_norm_g"])[0], 1024),
        "gng": bc(np.asarray(inputs["gnorm_g"])[0], 128),
        "ln1g": bc(np.asarray(inputs["ln1_g"])[0], D),
        "ln1b": bc(np.asarray(inputs["ln1_b"])[0], D),
        "ln2g": bc(np.asarray(inputs["ln2_g"])[0], D),
        "ln2b": bc(np.asarray(inputs["ln2_b"])[0], D),
        "cmask": cmask, "rmask": rmask, "ident": ident,
    }
    maps = []
    for core in range(8):
        b, half = core // 2, core % 2
        if half == 0:
            xe = np.concatenate([np.zeros((1024, D), np.float32), x[b, 0:1024]], axis=0)
            v = np.concatenate([np.zeros((128, 8), np.float32), np.ones((128, 8), np.float32)], axis=1)
        else:
            xe = x[b]
            v = np.ones((128, NT), np.float32)
        m = dict(shared)
        m["xs"] = np.ascontiguousarray(xe)
        m["valid"] = np.ascontiguousarray(v)
        m["cfm"] = np.ascontiguousarray(c[b].reshape(16, 128).T)
        maps.append(m)
    return maps


def kernel(**inputs):
    maps = _host_inputs(inputs)
    nc = build_program()
    res = run_bass_kernel_spmd(nc, maps, core_ids=list(range(8)))
    out = np.zeros((NB, SEQ, D), np.float32)
    for core in range(8):
        b, half = core // 2, core % 2
        out[b, half * 1024:(half + 1) * 1024] = res.results[core]["out"]
    return out
```
